# Optimizing a Trainium2 kernel written in Bass

```python
import math
import jax, jax.numpy as jnp
from jax import lax
import numpy as np

D_MODEL = 2048
BATCH = 2
SEQ = 4096
DEPTH = 4

N_MIXERS = 4
PLE_DIM = 256
EPS = 1e-6
BLOCK = 128

SWA_HEADS = 32
SWA_KV_HEADS = 4
SWA_HEAD_DIM = 64
SWA_GROUP = SWA_HEADS // SWA_KV_HEADS
SWA_WIDTH = SWA_HEADS * SWA_HEAD_DIM
SWA_KV_WIDTH = SWA_KV_HEADS * SWA_HEAD_DIM
WINDOW = 128
REL_BUCKETS = 32
REL_MAX_DIST = 128

CONV_WIDTH = D_MODEL
CONV_TAPS = 3

SSM_WIDTH = D_MODEL
SSM_GROUP = 16
SSM_STATE = 64
SSM_GROUPS = SSM_WIDTH // SSM_GROUP
DT_MIN = 1e-3
DT_MAX = 1e-1

FOX_HEADS = 32
FOX_HEAD_DIM = 64
FOX_WIDTH = FOX_HEADS * FOX_HEAD_DIM

kernel_name = 'hybrid_swa_conv_s5_fox_trunk'


def n_layers_of(m):
    return len(range(m, DEPTH, N_MIXERS))


def rmsnorm(x, g):
    x32 = x.astype(jnp.float32)
    y = x32 * lax.rsqrt(jnp.mean(x32 * x32, axis=-1, keepdims=True) + EPS)
    return (y * g.astype(jnp.float32)).astype(x.dtype)


def t5_bucket(dist):
    max_exact = REL_BUCKETS // 2
    d = np.maximum(dist, 1).astype(np.float32)
    large = max_exact + (np.log(d / max_exact) / np.log(REL_MAX_DIST / max_exact) * (REL_BUCKETS - max_exact)).astype(np.int32)
    large = np.minimum(large, REL_BUCKETS - 1)
    return np.where(dist < max_exact, dist, large).astype(np.int32)


def swa_mixer(h, w_in, w_out, sinks, rel_bias):
    bsz, seq, _ = h.shape
    nb = seq // BLOCK
    q, k, v, gate = jnp.split(h @ w_in, [SWA_WIDTH, SWA_WIDTH + SWA_KV_WIDTH, SWA_WIDTH + 2 * SWA_KV_WIDTH], axis=-1)
    q = q.reshape(bsz, nb, BLOCK, SWA_KV_HEADS, SWA_GROUP, SWA_HEAD_DIM)
    k = k.reshape(bsz, nb, BLOCK, SWA_KV_HEADS, SWA_HEAD_DIM)
    v = v.reshape(bsz, nb, BLOCK, SWA_KV_HEADS, SWA_HEAD_DIM)
    prev = lambda t: jnp.concatenate([jnp.zeros_like(t[:, :1]), t[:, :-1]], axis=1)
    kb = jnp.concatenate([prev(k), k], axis=2)
    vb = jnp.concatenate([prev(v), v], axis=2)
    qi = np.arange(BLOCK)[:, None]
    kj = np.arange(2 * BLOCK)[None, :]
    dist = qi + BLOCK - kj
    band = (dist >= 0) & (dist < WINDOW)
    exists = (np.arange(nb)[:, None, None] * BLOCK - BLOCK + kj[None]) >= 0
    mask = jnp.asarray(band[None] & exists)[None, :, None, None]
    bucket = t5_bucket(np.clip(dist, 0, None))
    bias = rel_bias.astype(jnp.float32)[bucket]
    bias = bias.transpose(2, 0, 1).reshape(SWA_KV_HEADS, SWA_GROUP, BLOCK, 2 * BLOCK)
    scores = jnp.einsum('bnqhgd,bnjhd->bnhgqj', q, kb).astype(jnp.float32) * (SWA_HEAD_DIM ** -0.5) + bias
    scores = jnp.where(mask, scores, jnp.finfo(jnp.float32).min)
    sink = sinks.astype(jnp.float32).reshape(SWA_KV_HEADS, SWA_GROUP)[:, :, None, None]
    m = jnp.maximum(scores.max(axis=-1, keepdims=True), sink)
    e = jnp.exp(scores - m)
    probs = e / (e.sum(axis=-1, keepdims=True) + jnp.exp(sink - m))
    out = jnp.einsum('bnhgqj,bnjhd->bnqhgd', probs.astype(vb.dtype), vb).reshape(bsz, seq, SWA_WIDTH)
    return (out * jax.nn.silu(gate)) @ w_out


def conv_mixer(h, w_in, conv_kernel, w_out):
    seq = h.shape[1]
    bg, cg, u, gate = jnp.split(h @ w_in, 4, axis=-1)
    z = cg * u
    zp = jnp.pad(z, ((0, 0), (CONV_TAPS - 1, 0), (0, 0)))
    conv = zp[:, 0:seq] * conv_kernel[0]
    for tap in range(1, CONV_TAPS):
        conv = conv + zp[:, tap:tap + seq] * conv_kernel[tap]
    y = bg * conv
    return (y * jax.nn.silu(gate)) @ w_out


def _complex_linear_combine(left, right):
    a1r, a1i, b1r, b1i = left
    a2r, a2i, b2r, b2i = right
    return (a1r * a2r - a1i * a2i, a1r * a2i + a1i * a2r,
            a2r * b1r - a2i * b1i + b2r, a2r * b1i + a2i * b1r + b2i)


def ssm_mixer(h, w_in, lam_re, lam_im, log_dt, b_re, b_im, c_re, c_im, d_skip, w_glu, b_glu, w_out):
    f32 = jnp.float32
    bsz, seq, _ = h.shape
    u, gate = jnp.split(h @ w_in, 2, axis=-1)
    u32 = u.astype(f32)
    ug = u32.reshape(bsz, seq, SSM_GROUPS, SSM_GROUP)
    dt = jnp.exp(log_dt.astype(f32))[:, None]
    lr = lam_re.astype(f32)
    li = lam_im.astype(f32)
    mag = jnp.exp(lr * dt)
    ab_re = mag * jnp.cos(li * dt)
    ab_im = mag * jnp.sin(li * dt)
    den = lr * lr + li * li
    nr = ab_re - 1.0
    coef_re = ((nr * lr + ab_im * li) / den)[..., None]
    coef_im = ((ab_im * lr - nr * li) / den)[..., None]
    br = b_re.astype(f32)
    bi = b_im.astype(f32)
    bb_re = coef_re * br - coef_im * bi
    bb_im = coef_re * bi + coef_im * br
    bu_re = jnp.einsum('bsgc,gnc->bsgn', ug, bb_re)
    bu_im = jnp.einsum('bsgc,gnc->bsgn', ug, bb_im)
    a_re = jnp.broadcast_to(ab_re[None, None], (1, seq, SSM_GROUPS, SSM_STATE))
    a_im = jnp.broadcast_to(ab_im[None, None], (1, seq, SSM_GROUPS, SSM_STATE))
    _, _, s_re, s_im = lax.associative_scan(_complex_linear_combine, (a_re, a_im, bu_re, bu_im), axis=1)
    y = jnp.einsum('bsgn,gcn->bsgc', s_re, c_re.astype(f32)) - jnp.einsum('bsgn,gcn->bsgc', s_im, c_im.astype(f32))
    y = y.reshape(bsz, seq, SSM_WIDTH) + d_skip.astype(f32) * u32
    y = jax.nn.gelu(y)
    ga, gb = jnp.split(y @ w_glu.astype(f32) + b_glu.astype(f32), 2, axis=-1)
    y = (ga * jax.nn.sigmoid(gb)).astype(h.dtype)
    return (y * jax.nn.silu(gate)) @ w_out


def fox_mixer(h, w_in, w_fg, b_fg, w_out):
    f32 = jnp.float32
    bsz, seq, _ = h.shape
    nb = seq // BLOCK
    q, k, v, gate = jnp.split(h @ w_in, 4, axis=-1)
    q = q.reshape(bsz, nb, BLOCK, FOX_HEADS, FOX_HEAD_DIM).transpose(1, 0, 2, 3, 4)
    k = k.reshape(bsz, seq, FOX_HEADS, FOX_HEAD_DIM)
    v = v.reshape(bsz, seq, FOX_HEADS, FOX_HEAD_DIM)
    log_f = jax.nn.log_sigmoid((h @ w_fg).astype(f32) + b_fg.astype(f32))
    csum = jnp.cumsum(log_f, axis=1)
    c_keys = csum.transpose(0, 2, 1)[:, :, None, :]
    c_q = csum.reshape(bsz, nb, BLOCK, FOX_HEADS).transpose(1, 0, 3, 2)
    kpos = jnp.arange(seq)
    neg = jnp.finfo(f32).min

    def block(args):
        n, qn, cn = args
        s = jnp.einsum('bqhd,bkhd->bhqk', qn, k).astype(f32) * (FOX_HEAD_DIM ** -0.5)
        s = s + cn[..., None] - c_keys
        qpos = n * BLOCK + jnp.arange(BLOCK)
        s = jnp.where(kpos[None, :] <= qpos[:, None], s, neg)
        pr = jax.nn.softmax(s, axis=-1)
        return jnp.einsum('bhqk,bkhd->bqhd', pr.astype(v.dtype), v)

    out = lax.map(block, (jnp.arange(nb), q, c_q))
    out = out.transpose(1, 0, 2, 3, 4).reshape(bsz, seq, FOX_WIDTH)
    return (out * jax.nn.silu(gate)) @ w_out


def setup_inputs(seed: int = 0) -> dict:
    key = jax.random.key(seed)
    ks = iter(jax.random.split(key, 40))
    f32 = jnp.float32
    nrm = lambda shape, scale: jax.random.normal(next(ks), shape, f32) * scale
    na, nc, ns, nf = (n_layers_of(m) for m in range(N_MIXERS))
    return {
        'x': nrm((BATCH, SEQ, D_MODEL), 1.0),
        'p': nrm((DEPTH, BATCH, SEQ, PLE_DIM), 1.0),
        'norm_g': 1.0 + nrm((DEPTH, D_MODEL), 0.01),
        'final_g': 1.0 + nrm((D_MODEL,), 0.01),
        'rel_bias': nrm((REL_BUCKETS, SWA_HEADS), 0.5),
        'swa_w_in': nrm((na, D_MODEL, 2 * SWA_WIDTH + 2 * SWA_KV_WIDTH), D_MODEL ** -0.5),
        'swa_w_out': nrm((na, SWA_WIDTH, D_MODEL), SWA_WIDTH ** -0.5),
        'swa_sinks': nrm((na, SWA_HEADS), 1.0),
        'conv_w_in': nrm((nc, D_MODEL, 4 * CONV_WIDTH), D_MODEL ** -0.5),
        'conv_kernel': nrm((nc, CONV_TAPS, CONV_WIDTH), CONV_TAPS ** -0.5),
        'conv_w_out': nrm((nc, CONV_WIDTH, D_MODEL), CONV_WIDTH ** -0.5),
        'ssm_w_in': nrm((ns, D_MODEL, 2 * SSM_WIDTH), D_MODEL ** -0.5),
        'ssm_lam_re': -0.5 + nrm((ns, SSM_GROUPS, SSM_STATE), 0.01),
        'ssm_lam_im': jnp.broadcast_to(math.pi * jnp.arange(SSM_STATE, dtype=f32), (ns, SSM_GROUPS, SSM_STATE)) + nrm((ns, SSM_GROUPS, SSM_STATE), 0.01),
        'ssm_log_dt': jax.random.uniform(next(ks), (ns, SSM_GROUPS), f32, math.log(DT_MIN), math.log(DT_MAX)),
        'ssm_b_re': nrm((ns, SSM_GROUPS, SSM_STATE, SSM_GROUP), (2 * SSM_GROUP) ** -0.5),
        'ssm_b_im': nrm((ns, SSM_GROUPS, SSM_STATE, SSM_GROUP), (2 * SSM_GROUP) ** -0.5),
        'ssm_c_re': nrm((ns, SSM_GROUPS, SSM_GROUP, SSM_STATE), SSM_STATE ** -0.5),
        'ssm_c_im': nrm((ns, SSM_GROUPS, SSM_GROUP, SSM_STATE), SSM_STATE ** -0.5),
        'ssm_d': nrm((ns, SSM_WIDTH), 1.0),
        'ssm_w_glu': nrm((ns, SSM_WIDTH, 2 * SSM_WIDTH), SSM_WIDTH ** -0.5),
        'ssm_b_glu': nrm((ns, 2 * SSM_WIDTH), 0.01),
        'ssm_w_out': nrm((ns, SSM_WIDTH, D_MODEL), SSM_WIDTH ** -0.5),
        'fox_w_in': nrm((nf, D_MODEL, 4 * FOX_WIDTH), D_MODEL ** -0.5),
        'fox_w_fg': nrm((nf, D_MODEL, FOX_HEADS), D_MODEL ** -0.5),
        'fox_b_fg': jax.random.uniform(next(ks), (nf, FOX_HEADS), f32, 1.0, 5.0),
        'fox_w_out': nrm((nf, FOX_WIDTH, D_MODEL), FOX_WIDTH ** -0.5),
        'ple_proj': nrm((DEPTH, PLE_DIM, D_MODEL), PLE_DIM ** -0.5),
        'ple_norm': 1.0 + nrm((DEPTH, D_MODEL), 0.01),
        'ple_gate': nrm((DEPTH, D_MODEL, D_MODEL), D_MODEL ** -0.5),
    }


def reference(x, p, norm_g, final_g, rel_bias, swa_w_in, swa_w_out, swa_sinks, conv_w_in, conv_kernel, conv_w_out, ssm_w_in, ssm_lam_re, ssm_lam_im, ssm_log_dt, ssm_b_re, ssm_b_im, ssm_c_re, ssm_c_im, ssm_d, ssm_w_glu, ssm_b_glu, ssm_w_out, fox_w_in, fox_w_fg, fox_b_fg, fox_w_out, ple_proj, ple_norm, ple_gate):
    for i in range(DEPTH):
        mixer, j = i % N_MIXERS, i // N_MIXERS
        hn = rmsnorm(x, norm_g[i])
        if mixer == 0:
            y = swa_mixer(hn, swa_w_in[j], swa_w_out[j], swa_sinks[j], rel_bias)
        elif mixer == 1:
            y = conv_mixer(hn, conv_w_in[j], conv_kernel[j], conv_w_out[j])
        elif mixer == 2:
            y = ssm_mixer(hn, ssm_w_in[j], ssm_lam_re[j], ssm_lam_im[j], ssm_log_dt[j], ssm_b_re[j], ssm_b_im[j], ssm_c_re[j], ssm_c_im[j], ssm_d[j], ssm_w_glu[j], ssm_b_glu[j], ssm_w_out[j])
        else:
            y = fox_mixer(hn, fox_w_in[j], fox_w_fg[j], fox_b_fg[j], fox_w_out[j])
        x = x + y
        emb = p[i] @ ple_proj[i]
        g = jax.nn.sigmoid(rmsnorm(x, ple_norm[i]) @ ple_gate[i])
        x = x + emb * g
    return rmsnorm(x, final_g)
```

```python
import math
import numpy as np
import ml_dtypes
import concourse.bass as bass
import concourse.mybir as mybir
from concourse.bass_utils import run_bass_kernel_spmd

F32 = mybir.dt.float32
BF16 = mybir.dt.bfloat16
I32 = mybir.dt.int32
AF = mybir.ActivationFunctionType
ALU = mybir.AluOpType
NPBF = ml_dtypes.bfloat16

D = 2048
DC = 16
T = 1024
SEQ = 4096
NCORE = 8
EPS = 1e-6
ENG_NAMES = ["pe", "act", "dve", "pool", "sp"]


class Tok:
    __slots__ = ("key", "val")

    def __init__(self, key, val):
        self.key = key
        self.val = val


class Prog:
    def __init__(self):
        self.nc = bass.Bass("TRN2", target_bir_lowering=False)
        self.q = {n: [] for n in ENG_NAMES}
        self.cnt = {}
        self.seen = {n: {} for n in ENG_NAMES}
        self.sem_keys = []
        self._ctx = []
        for n in ["pe", "act", "dve", "pool"]:
            self._newsem(n)

    def _newsem(self, key):
        if key not in self.cnt:
            self.cnt[key] = 0
            self.sem_keys.append(key)

    def sb(self, name, shape, dt):
        cm = self.nc.sbuf_tensor(name, list(shape), dt)
        t = cm.__enter__()
        self._ctx.append(cm)
        return t

    def ps(self, name, shape, dt=F32):
        cm = self.nc.psum_tensor(name, list(shape), dt)
        t = cm.__enter__()
        self._ctx.append(cm)
        return t

    def dram(self, name, shape, dt, kind=None):
        if kind is None:
            return self.nc.dram_tensor(name, list(shape), dt)
        return self.nc.dram_tensor(name, list(shape), dt, kind=kind)

    def _waits(self, engine, deps):
        ws = []
        seen = self.seen[engine]
        for d in deps:
            if d is None:
                continue
            if isinstance(d, (list, tuple)):
                ws += self._waits(engine, d)
                continue
            if seen.get(d.key, 0) >= d.val:
                continue
            seen[d.key] = d.val
            ws.append((d.key, d.val))
        return ws

    def op(self, engine, fn, deps=()):
        ws = self._waits(engine, deps)
        self.cnt[engine] += 1
        v = self.cnt[engine]
        self.q[engine].append((fn, ws, engine, 1))
        return Tok(engine, v)

    def dma(self, engine, semkey, out, in_, deps=(), **kw):
        self._newsem(semkey)
        ws = self._waits(engine, deps)
        self.cnt[semkey] += 16
        v = self.cnt[semkey]
        self.q[engine].append((lambda e: e.dma_start(out=out, in_=in_, **kw), ws, semkey, 16))
        return Tok(semkey, v)

    def wait(self, engine, deps):
        ws = self._waits(engine, deps)
        if ws:
            self.q[engine].append((None, ws, None, 0))

    def build(self):
        nc = self.nc
        sems = {}
        cms = []
        for k in self.sem_keys:
            cm = nc.semaphore("s_" + str(k))
            sems[k] = cm.__enter__()
            cms.append(cm)
        q = self.q
        with nc.Block() as block:
            def emit(name):
                def body(e):
                    for fn, ws, key, inc in q[name]:
                        for (wk, wv) in ws:
                            e.wait_ge(sems[wk], wv)
                        if fn is not None:
                            ins = fn(e)
                            ins.then_inc(sems[key], inc)
                return body
            if q["pe"]:
                block.tensor(emit("pe"))
            if q["act"]:
                block.scalar(emit("act"))
            if q["dve"]:
                block.vector(emit("dve"))
            if q["pool"]:
                block.gpsimd(emit("pool"))
            if q["sp"]:
                block.sync(emit("sp"))
        for cm in reversed(cms):
            cm.__exit__(None, None, None)
        for cm in reversed(self._ctx):
            cm.__exit__(None, None, None)
        return nc


class KB:
    def __init__(self, nslots=6, wcols=256):
        self.P = Prog()
        P = self.P
        self.psum = P.ps("psum", [128, 8, 512], F32)
        self.bank_tok = [None] * 8
        self.bank_rr = 0
        self.ones = P.sb("ones_bf", [128, 128], BF16)
        self.t_ones = P.op("dve", lambda e: e.memset(self.ones[:], 1.0))
        self.cst = P.sb("cst", [128, 8], F32)
        cvals = [EPS, math.pi / 2, 1.0, 0.0, -1.0, 0.5, 2.0, -math.pi / 2]
        tc = None
        for i, v in enumerate(cvals):
            tc = P.op("dve", lambda e, i=i, v=v: e.memset(self.cst[:, i:i + 1], v))
        self.t_cst = tc
        self.wcols = wcols
        self.wbuf = [P.sb(f"w{i}", [128, 16, wcols], BF16) for i in range(nslots)]
        self.w_free = [None] * nslots
        self.w_rr = 0
        self.sq = [P.sb(f"sq{i}", [128, 512], BF16) for i in range(2)]
        self.sq_free = [None, None]
        self.sq_rr = 0
        self.rs = [P.sb(f"rs{i}", [128, 512], F32) for i in range(2)]
        self.rs_free = [None, None]
        self.rs_rr = 0
        self.out_toks = []
        self.io_n = 0
        self.scr = P.sb("scr", [128, 2048], F32)

    def scrv(self, i):
        return self.scr[:, i * 512:(i + 1) * 512]

    def bank(self):
        b = self.bank_rr
        self.bank_rr = (b + 1) % 8
        return b, self.bank_tok[b]

    def load(self, out, in_, deps=(), eng="sp", key=None):
        if key is None:
            key = f"io{self.io_n}"
            self.io_n += 1
        return self.P.dma(eng, key, out, in_, deps)

    def wstream(self, loads):
        P = self.P
        ns = len(self.wbuf)
        pre = ns - 1
        state = {"next": 0, "info": {}}

        def issue(i):
            Wd, row0, K, col0, ncols = loads[i][:5]
            s = self.w_rr
            self.w_rr = (s + 1) % ns
            KC = K // 128
            src = Wd.ap()[row0:row0 + K, col0:col0 + ncols].rearrange("(kc p) n -> p kc n", p=128)
            t = P.dma("pool", f"w{s}", self.wbuf[s][:, 0:KC, 0:ncols], src, deps=[self.w_free[s]])
            state["info"][i] = (s, t)

        def get(i, done=0):
            while state["next"] < len(loads) and (state["next"] <= i or (state["next"] <= i + pre - 1 and state["next"] - ns < done)):
                issue(state["next"])
                state["next"] += 1
            return state["info"][i]

        return get

    def wdone(self, slot, tok):
        self.w_free[slot] = tok

    def gemm(self, specs, ngroups, tblocks, epi, gcols=None):
        P = self.P
        gcols = gcols or self.wcols
        loads = []
        for g in range(ngroups):
            for sp in specs:
                loads.append((sp["W"], sp.get("row0", 0), sp["K"], sp["col"](g), gcols))
        get = self.wstream(loads)
        nsp = len(specs)
        nch = (gcols + 127) // 128
        for g in range(ngroups):
            infos = [get(g * nsp + si, g * nsp) for si in range(nsp)]
            lasts = [None] * nsp
            for ci in range(nch):
                cw = min(128, gcols - ci * 128)
                for bi, (t0, n) in enumerate(tblocks):
                    pas, toks, banks = [], [], []
                    for si, sp in enumerate(specs):
                        s, wt = infos[si]
                        KC = sp["K"] // 128
                        b, bdep = self.bank()
                        pa = self.psum[0:cw, b, 0:n]
                        last = None
                        for kc in range(KC):
                            deps = [wt, bdep] + list(sp.get("deps", [])) if kc == 0 else []
                            last = P.op("pe", (lambda e, pa=pa, s=s, kc=kc, ci=ci, cw=cw, t0=t0, n=n, sp=sp, KC=KC:
                                               e.matmul(pa, self.wbuf[s][:, kc, ci * 128:ci * 128 + cw], sp["rhs"](kc, t0, n),
                                                        start=(kc == 0), stop=(kc == KC - 1))), deps)
                        lasts[si] = last
                        pas.append(pa)
                        toks.append(last)
                        banks.append(b)
                    rel = epi(g, ci, bi, pas, toks)
                    for b, r in zip(banks, rel):
                        self.bank_tok[b] = r
            for si in range(nsp):
                self.wdone(infos[si][0], lasts[si])

    def rmsnorm(self, Xf, g_sb, outf, tblocks, deps=(), out_deps=()):
        P = self.P
        last = None
        for (t0, n) in tblocks:
            b, bdep = self.bank()
            pa = self.psum[:, b, 0:n]
            tm = None
            for c in range(DC):
                i = self.sq_rr
                self.sq_rr = 1 - i
                sq = self.sq[i]
                ta = P.op("act", lambda e, sq=sq, c=c, t0=t0, n=n: e.activation(out=sq[:, 0:n], in_=Xf(c, t0, n), func=AF.Square),
                          [deps, self.sq_free[i]])
                tm = P.op("pe", lambda e, pa=pa, sq=sq, c=c, n=n: e.matmul(pa, self.ones[:], sq[:, 0:n], start=(c == 0), stop=(c == DC - 1)),
                          [ta, self.t_ones] + ([bdep] if c == 0 else []))
                self.sq_free[i] = tm
            j = self.rs_rr
            self.rs_rr = 1 - j
            rs = self.rs[j]
            t1 = P.op("act", lambda e, rs=rs, pa=pa, n=n: e.activation(out=rs[:, 0:n], in_=pa, func=AF.Sqrt, scale=1.0 / D, bias=self.cst[:, 0:1]),
                      [tm, self.rs_free[j], self.t_cst])
            self.bank_tok[b] = t1
            t2 = P.op("dve", lambda e, rs=rs, n=n: e.reciprocal(out=rs[:, 0:n], in_=rs[:, 0:n]), [t1])
            for c in range(DC):
                last = P.op("dve", lambda e, rs=rs, c=c, t0=t0, n=n: e.scalar_tensor_tensor(
                    out=outf(c, t0, n), in0=Xf(c, t0, n), scalar=g_sb[:, c:c + 1], in1=rs[:, 0:n], op0=ALU.mult, op1=ALU.mult),
                    [t2, deps, out_deps])
            self.rs_free[j] = last
        return last

    def outproj_residual(self, Wd, X, A, tblocks, a_deps, x_deps=()):
        P = self.P
        st = {"last": None}

        def epi(g, ci, bi, pas, toks):
            j = g * 2 + ci
            t0, n = tblocks[bi]
            t = P.op("dve", lambda e: e.tensor_tensor(out=X[:, j, t0:t0 + n], in0=X[:, j, t0:t0 + n], in1=pas[0], op=ALU.add),
                     [toks[0], x_deps])
            st["last"] = t
            return [t]

        self.gemm([dict(W=Wd, K=D, col=lambda g: g * 256, rhs=lambda kc, t0, n: A[:, kc, t0:t0 + n], deps=[a_deps])],
                  8, tblocks, epi)
        return st["last"]

    def ple(self, Wproj, Wgate, li, pn_sb, X, A, pT, tblocks, x_deps, a_free, p_deps):
        P = self.P
        tn = self.rmsnorm(lambda c, t0, n: X[:, c, t0:t0 + n], pn_sb, lambda c, t0, n: A[:, c, t0:t0 + n], tblocks,
                          deps=[x_deps], out_deps=[a_free])
        tmp = [self.scrv(0), self.scrv(1)]
        st = {"rr": 0, "free": [None, None], "last": None}

        def epi(g, ci, bi, pas, toks):
            j = g * 2 + ci
            t0, n = tblocks[bi]
            i = st["rr"]
            st["rr"] = 1 - i
            tb = tmp[i]
            t1 = P.op("act", lambda e: e.activation(out=tb[:, 0:n], in_=pas[0], func=AF.Sigmoid), [toks[0], st["free"][i]])
            t2 = P.op("dve", lambda e: e.tensor_tensor(out=tb[:, 0:n], in0=pas[1], in1=tb[:, 0:n], op=ALU.mult), [t1, toks[1]])
            t3 = P.op("dve", lambda e: e.tensor_tensor(out=X[:, j, t0:t0 + n], in0=X[:, j, t0:t0 + n], in1=tb[:, 0:n], op=ALU.add), [t2])
            st["free"][i] = t3
            st["last"] = t3
            return [t1, t2]

        self.gemm([dict(W=Wgate, K=D, row0=0, col=lambda g: g * 256, rhs=lambda kc, t0, n: A[:, kc, t0:t0 + n], deps=[tn]),
                   dict(W=Wproj, K=256, row0=0, col=lambda g: g * 256, rhs=lambda kc, t0, n: pT[:, kc, t0:t0 + n], deps=[p_deps])],
                  8, tblocks, epi)
        return st["last"]

    def finish(self):
        self.P.wait("sp", self.out_toks)
        return self.P.build()


TB2 = [(0, 512), (512, 512)]


def fm(a):
    t, f = a.shape
    return np.ascontiguousarray(a.T.reshape(f // 128, 128, t).transpose(1, 0, 2))


def unfm(a):
    p, c, t = a.shape
    return np.ascontiguousarray(a.transpose(1, 0, 2).reshape(c * p, t).T)


def vec_fm(v):
    return np.ascontiguousarray(v.reshape(-1, 128).T)


def build_l0(stop=None):
    kb = KB(nslots=3)
    P = kb.P
    HL = 128
    TE = T + HL
    xin = P.dram("xT", [128, DC, TE], F32, kind="ExternalInput")
    pin = P.dram("pT", [128, 2, T], F32, kind="ExternalInput")
    w_in = P.dram("w_in", [D, 4608], F32, kind="ExternalInput")
    w_out = P.dram("w_out", [D, D], F32, kind="ExternalInput")
    w_pp = P.dram("ple_proj", [256, D], F32, kind="ExternalInput")
    w_pg = P.dram("ple_gate", [D, D], F32, kind="ExternalInput")
    ng_d = P.dram("norm_g", [128, DC], F32, kind="ExternalInput")
    pn_d = P.dram("ple_norm", [128, DC], F32, kind="ExternalInput")
    bias_d = P.dram("biasT", [128, 2, 32, 128], F32, kind="ExternalInput")
    mask_d = P.dram("maskT", [128, 2, 128], F32, kind="ExternalInput")
    sink_d = P.dram("sinks", [128, 32], F32, kind="ExternalInput")
    hp_d = P.dram("hasprev", [128, 1], F32, kind="ExternalInput")
    xout = P.dram("xo", [128, DC, T], F32, kind="ExternalOutput")

    X = P.sb("X", [128, DC, T], F32)
    A = P.sb("A", [128, DC, TE], BF16)
    B = P.sb("B", [128, DC, T], BF16)
    Xh = kb.scr[:].rearrange("p (c t) -> p c t", c=DC)
    KT = P.sb("KT", [128, 2, TE], BF16)
    KA = P.sb("KA", [128, 2, TE], BF16)
    VE = P.sb("VE", [128, 9, 4, 128], BF16)
    M = X[:, 0:8, :].rearrange("p c t -> p (c t)").rearrange("p (k h q) -> p k h q", k=2, h=32)
    ng = P.sb("ng", [128, DC], F32)
    pn = P.sb("pn", [128, DC], F32)
    esk = P.sb("esk", [128, 32], F32)
    hp = P.sb("hp", [128, 1], F32)
    mk = P.sb("mk", [128, 2, 128], F32)
    pT = P.sb("pTb", [128, 2, T], BF16)

    tx = kb.load(X[:], xin.ap()[:, :, HL:TE])
    txh = kb.load(Xh[:], xin.ap()[:, :, 0:HL])
    tng = kb.load(ng[:], ng_d.ap())
    tpn = kb.load(pn[:], pn_d.ap())
    tmk = kb.load(mk[:], mask_d.ap())
    tsk = kb.load(esk[:], sink_d.ap())
    thp = kb.load(hp[:], hp_d.ap())
    tp = P.dma("pool", "pcast", pT[:], pin.ap())

    tsk = P.op("act", lambda e: e.activation(out=esk[:], in_=esk[:], func=AF.Exp), [tsk])
    tve = P.op("dve", lambda e: e.memset(VE[:], 1.0))

    tnh = kb.rmsnorm(lambda c, t0, n: Xh[:, c, t0:t0 + n], ng, lambda c, t0, n: A[:, c, t0:t0 + n], [(0, HL)], deps=[txh, tng])
    tno = kb.rmsnorm(lambda c, t0, n: X[:, c, t0:t0 + n], ng, lambda c, t0, n: A[:, c, HL + t0:HL + t0 + n], TB2, deps=[tx, tng])
    tn = [tnh, tno]
    def early(deps):
        kb.out_toks.append(kb.load(xout.ap(), X[:], deps=deps))
        return kb.finish()
    if stop == "norm":
        return early([tnh, tno])
    rhsA_own = lambda kc, t0, n: A[:, kc, HL + t0:HL + t0 + n]
    rhsA_ext = lambda kc, t0, n: A[:, kc, t0:t0 + n]

    st = {"q": None, "k": None}

    def epi_q(g, ci, bi, pas, toks):
        j = g * 2 + ci
        t0, n = TB2[bi]
        t = P.op("act", lambda e: e.activation(out=B[:, j, t0:t0 + n], in_=pas[0], func=AF.Copy, scale=0.125), [toks[0]])
        st["q"] = t
        return [t]

    kb.gemm([dict(W=w_in, K=D, col=lambda g: g * 256, rhs=rhsA_own, deps=tn)], 8, TB2, epi_q)

    if stop == "q":
        return early([st["q"]])
    import os
    TBE = [(0, 512), (512, 512), (1024, 128)]
    if os.environ.get("KDBG") == "2blk":
        TBE = TBE[:2]

    def epi_k(g, ci, bi, pas, toks):
        t0, n = TBE[bi]
        t = P.op("act", lambda e: e.activation(out=KT[:, ci, t0:t0 + n], in_=pas[0], func=AF.Copy), [toks[0]])
        if os.environ.get("KDBG") == "noka":
            st["k"] = [t]
            return [t]
        t2 = P.op("act", lambda e: e.activation(out=KA[0:64, ci, t0:t0 + n], in_=KT[64:128, ci, t0:t0 + n], func=AF.Copy), [t])
        t3 = P.op("act", lambda e: e.activation(out=KA[64:128, ci, t0:t0 + n], in_=KT[0:64, ci, t0:t0 + n], func=AF.Copy), [t])
        st["k"] = [t, t3]
        return [[t, t3]]

    kb.gemm([dict(W=w_in, K=D, col=lambda g: 2048, rhs=rhsA_ext, deps=tn)], 1, TBE, epi_k)

    if stop == "k":
        return early([st["k"]])
    getv = kb.wstream([(w_in, 0, D, 2304, 256)])
    sv, tv = getv(0)
    lastv = None
    tvs = []
    for blk in range(9):
        b, bdep = kb.bank()
        pa = kb.psum[:, b, 0:256]
        for kc in range(DC):
            lastv = P.op("pe", lambda e, pa=pa, kc=kc, blk=blk: e.matmul(pa, A[:, kc, blk * 128:(blk + 1) * 128], kb.wbuf[sv][:, kc, 0:256],
                                                                      start=(kc == 0), stop=(kc == DC - 1)),
                         [tv, bdep] + tn if kc == 0 else [])
        tcp = P.op("act", lambda e, pa=pa, blk=blk: e.activation(out=VE[:, blk, :, 0:64], in_=pa.rearrange("p (k d) -> p k d", k=4), func=AF.Copy),
                   [lastv, tve])
        kb.bank_tok[b] = tcp
        tvs.append(tcp)
    kb.wdone(sv, lastv)

    if stop == "kv":
        return early([st["k"], tvs])
    tb_ = kb.load(M, bias_d.ap(), deps=[tno, st["q"]])
    t = P.op("act", lambda e: e.activation(out=M, in_=M, func=AF.Exp), [tb_])
    for kbk in range(2):
        for h in range(32):
            tM = P.op("dve", lambda e, kbk=kbk, h=h: e.tensor_tensor(out=M[:, kbk, h, :], in0=M[:, kbk, h, :], in1=mk[:, kbk, :], op=ALU.mult), [t, tmk])

    ATT_N = int(os.environ.get("ATT_N", "64"))
    ATT_PH = int(os.environ.get("ATT_PH", "9"))
    if ATT_PH == 0:
        return early([tM])
    Eb = [kb.scrv(0), kb.scrv(1)]
    Pb = [P.sb(f"Pm{i}", [128, 512], BF16) for i in range(4)]
    rc = [P.sb(f"rc{i}", [64, 128], F32) for i in range(4)]
    Efree = [None, None]
    Pfree = [None] * 4
    rcfree = [None] * 4
    rr = {"e": 0, "p": 0, "r": 0}
    att_last = None
    dbg_toks = []
    itn = 0
    for qb in range(8):
        for kvh in range(4):
            for gq in range(2):
                itn += 1
                if itn > ATT_N:
                    continue
                par = gq
                h0 = kvh * 8 + par * 4
                pms = []
                for kbk in range(2):
                    b, bdep = kb.bank()
                    last = None
                    for i in range(4):
                        h = kvh * 8 + 2 * i + par
                        c = h // 2
                        r0 = par * 64
                        ksrc = KT if (kvh % 2) * 64 == r0 else KA
                        blk = qb + kbk
                        last = P.op("pe", lambda e, b=b, i=i, c=c, r0=r0, ksrc=ksrc, blk=blk, kvh=kvh, qb=qb:
                                    e.matmul(kb.psum[:, b, i * 128:(i + 1) * 128], ksrc[r0:r0 + 64, kvh // 2, blk * 128:(blk + 1) * 128],
                                             B[r0:r0 + 64, c, qb * 128:(qb + 1) * 128], start=True, stop=True),
                                    [bdep, st["q"], st["k"]] if i == 0 else [])
                    ei = rr["e"]
                    rr["e"] = 1 - ei
                    E = Eb[ei]
                    te = P.op("act", lambda e, E=E, b=b: e.activation(out=E[:], in_=kb.psum[:, b, :], func=AF.Exp), [last, Efree[ei]])
                    kb.bank_tok[b] = te
                    dbg_toks.append(te)
                    if ATT_PH == 1:
                        continue
                    pi = rr["p"]
                    rr["p"] = (pi + 1) % 4
                    Pm = Pb[pi]
                    Mv = M[:, kbk, h0:h0 + 4, :].rearrange("p h q -> p (h q)")
                    if qb == 0 and kbk == 0:
                        tpm = P.op("dve", lambda e, Pm=Pm, E=E, Mv=Mv: e.scalar_tensor_tensor(out=Pm[:], in0=E[:], scalar=hp[:, 0:1], in1=Mv,
                                                                                           op0=ALU.mult, op1=ALU.mult),
                                   [te, tM, thp, Pfree[pi]])
                    else:
                        tpm = P.op("dve", lambda e, Pm=Pm, E=E, Mv=Mv: e.tensor_tensor(out=Pm[:], in0=E[:], in1=Mv, op=ALU.mult),
                                   [te, tM, Pfree[pi]])
                    Efree[ei] = tpm
                    dbg_toks.append(tpm)
                    pms.append((pi, Pm, tpm))
                if ATT_PH <= 2:
                    continue
                b, bdep = kb.bank()
                last = None
                for kbk in range(2):
                    pi, Pm, tpm = pms[kbk]
                    blk = qb + kbk
                    last = P.op("pe", lambda e, b=b, Pm=Pm, blk=blk, kvh=kvh, kbk=kbk:
                                e.matmul(kb.psum[:, b, :], VE[:, blk, kvh, :], Pm[:], start=(kbk == 0), stop=(kbk == 1)),
                                [tpm, bdep, tvs[blk]])
                for kbk in range(2):
                    Pfree[pms[kbk][0]] = last
                dbg_toks.append(last)
                if ATT_PH <= 3:
                    continue
                for i in range(4):
                    h = kvh * 8 + 2 * i + par
                    c = h // 2
                    r0 = par * 64
                    ri = rr["r"]
                    rr["r"] = (ri + 1) % 4
                    R = rc[ri]
                    t1 = P.op("dve", lambda e, R=R, b=b, i=i, h=h: e.tensor_scalar(out=R[:], in0=kb.psum[64:128, b, i * 128:(i + 1) * 128],
                                                                                 scalar1=esk[64:128, h:h + 1], scalar2=None, op0=ALU.add),
                              [last, tsk, rcfree[ri]])
                    t2 = P.op("dve", lambda e, R=R: e.reciprocal(out=R[:], in_=R[:]), [t1])
                    if r0 == 0:
                        t3 = P.op("dve", lambda e, R=R, b=b, i=i, c=c, r0=r0, qb=qb: e.tensor_tensor(
                            out=B[0:64, c, qb * 128:(qb + 1) * 128], in0=kb.psum[0:64, b, i * 128:(i + 1) * 128], in1=R[:], op=ALU.mult), [t2])
                        t4 = t3
                    else:
                        t3 = P.op("dve", lambda e, R=R, b=b, i=i: e.tensor_tensor(
                            out=R[:], in0=kb.psum[0:64, b, i * 128:(i + 1) * 128], in1=R[:], op=ALU.mult), [t2])
                        t4 = P.op("act", lambda e, R=R, c=c, qb=qb: e.activation(out=B[64:128, c, qb * 128:(qb + 1) * 128], in_=R[:], func=AF.Copy), [t3])
                    rcfree[ri] = t4
                    att_last = [t3, t4]
                kb.bank_tok[b] = att_last

    if stop == "att":
        tx2 = kb.load(X[:], xin.ap()[:, :, HL:TE], deps=[att_last, last, dbg_toks])
        return early([tx2])
    sg = [kb.scrv(2), kb.scrv(3)]
    last_pv = last
    stg = {"rr": 0, "free": [None, None], "last": None}

    def epi_g(g, ci, bi, pas, toks):
        j = g * 2 + ci
        t0, n = TB2[bi]
        i = stg["rr"]
        stg["rr"] = 1 - i
        S = sg[i]
        t1 = P.op("act", lambda e: e.activation(out=S[:, 0:n], in_=pas[0], func=AF.Silu), [toks[0], stg["free"][i]])
        t2 = P.op("dve", lambda e: e.tensor_tensor(out=B[:, j, t0:t0 + n], in0=B[:, j, t0:t0 + n], in1=S[:, 0:n], op=ALU.mult), [t1, att_last])
        stg["free"][i] = t2
        stg["last"] = t2
        return [t1]

    kb.gemm([dict(W=w_in, K=D, col=lambda g: 2560 + g * 256, rhs=rhsA_own, deps=tn)], 8, TB2, epi_g)

    if stop == "gate":
        tx2 = kb.load(X[:], xin.ap()[:, :, HL:TE], deps=[att_last, last_pv, stg["last"]])
        return early([tx2])
    tx2 = kb.load(X[:], xin.ap()[:, :, HL:TE], deps=[att_last, last_pv])
    tres = kb.outproj_residual(w_out, X, B, TB2, a_deps=stg["last"], x_deps=[tx2])
    if stop == "out":
        return early([tres])
    tple = kb.ple(w_pp, w_pg, 0, pn, X, A, pT, TB2, x_deps=[tres, tpn], a_free=[tres], p_deps=[tp])
    kb.out_toks.append(kb.load(xout.ap(), X[:], deps=[tple]))
    return kb.finish()


def build_l1():
    kb = KB(nslots=5)
    P = kb.P
    HL = 2
    TE = T + HL
    xin = P.dram("xT", [128, DC, TE], F32, kind="ExternalInput")
    pin = P.dram("pT", [128, 2, T], F32, kind="ExternalInput")
    w_in = P.dram("w_in", [D, 8192], F32, kind="ExternalInput")
    w_out = P.dram("w_out", [D, D], F32, kind="ExternalInput")
    w_pp = P.dram("ple_proj", [256, D], F32, kind="ExternalInput")
    w_pg = P.dram("ple_gate", [D, D], F32, kind="ExternalInput")
    ng_d = P.dram("norm_g", [128, DC], F32, kind="ExternalInput")
    pn_d = P.dram("ple_norm", [128, DC], F32, kind="ExternalInput")
    ck_d = P.dram("convk", [128, DC, 3], F32, kind="ExternalInput")
    w2_in = P.dram("w2_in", [D, 4096], F32, kind="ExternalInput")
    ng2_d = P.dram("norm_g2", [128, DC], F32, kind="ExternalInput")
    xout = P.dram("xo", [128, DC, T], F32, kind="ExternalOutput")
    uout = P.dram("uo", [128, DC, T], F32, kind="ExternalOutput")
    sout = P.dram("so", [128, DC, T], F32, kind="ExternalOutput")

    X = P.sb("X", [128, DC, T], F32)
    A = P.sb("A", [128, DC, TE], BF16)
    Braw = P.sb("Braw", [128, 8192], F32)
    B = Braw[:].bitcast(BF16).rearrange("p (c t) -> p c t", c=DC)
    ST = [Braw[:, i * 1024:(i + 1) * 1024] for i in range(8)]
    Xh = P.sb("Xh", [128, DC, HL], F32)
    Z = [P.sb(f"Z{i}", [128, 2, TE], F32) for i in range(2)]
    ng = P.sb("ng", [128, DC], F32)
    pn = P.sb("pn", [128, DC], F32)
    ng2 = P.sb("ng2", [128, DC], F32)
    ck = P.sb("ck", [128, DC, 3], F32)
    pT = P.sb("pTb", [128, 2, T], BF16)

    tx = kb.load(X[:], xin.ap()[:, :, HL:TE])
    txh = kb.load(Xh[:], xin.ap()[:, :, 0:HL])
    tng = kb.load(ng[:], ng_d.ap())
    tpn = kb.load(pn[:], pn_d.ap())
    tng2 = kb.load(ng2[:], ng2_d.ap())
    tck = kb.load(ck[:], ck_d.ap())
    tp = P.dma("pool", "pcast", pT[:], pin.ap())

    tnh = kb.rmsnorm(lambda c, t0, n: Xh[:, c, t0:t0 + n], ng, lambda c, t0, n: A[:, c, t0:t0 + n], [(0, HL)], deps=[txh, tng])
    tno = kb.rmsnorm(lambda c, t0, n: X[:, c, t0:t0 + n], ng, lambda c, t0, n: A[:, c, HL + t0:HL + t0 + n], TB2, deps=[tx, tng])
    tn = [tnh, tno]
    rhsA_ext = lambda kc, t0, n: A[:, kc, t0:t0 + n]
    TBE = [(0, HL), (HL, 512), (HL + 512, 512)]
    tA, tB_, tC = kb.scrv(0), kb.scrv(1), kb.scrv(2)
    stc = {"fa": None, "fb": None, "fc": None, "z": {}, "last": None, "zfree": [None, None]}

    def epi_conv(g, ci, bi, pas, toks):
        j = g * 2 + ci
        t0, n = TBE[bi]
        Zt = Z[g % 2]
        t1 = P.op("act", lambda e: e.activation(out=tA[:, 0:n], in_=pas[0], func=AF.Copy), [toks[0], stc["fa"]])
        zdeps = [t1, toks[1]]
        if bi == 0 and ci == 0:
            zdeps.append(stc["zfree"][g % 2])
        t2 = P.op("dve", lambda e: e.tensor_tensor(out=Zt[:, ci, t0:t0 + n], in0=pas[1], in1=tA[:, 0:n], op=ALU.mult), zdeps)
        stc["fa"] = t2
        stc["z"][(g, ci, bi)] = t2
        if bi == 0:
            return [t1, t2, toks[2], toks[3]]
        zp = stc["z"][(g, ci, bi - 1)]
        t3 = P.op("dve", lambda e: e.tensor_scalar(out=tB_[:, 0:n], in0=Zt[:, ci, t0:t0 + n], scalar1=ck[:, j, 2:3], scalar2=None, op0=ALU.mult),
                  [t2, zp, tck, stc["fb"]])
        t4 = P.op("dve", lambda e: e.scalar_tensor_tensor(out=tB_[:, 0:n], in0=Zt[:, ci, t0 - 1:t0 - 1 + n], scalar=ck[:, j, 1:2], in1=tB_[:, 0:n],
                                                         op0=ALU.mult, op1=ALU.add), [t3])
        t5 = P.op("dve", lambda e: e.scalar_tensor_tensor(out=tB_[:, 0:n], in0=Zt[:, ci, t0 - 2:t0 - 2 + n], scalar=ck[:, j, 0:1], in1=tB_[:, 0:n],
                                                         op0=ALU.mult, op1=ALU.add), [t4])
        t6 = P.op("dve", lambda e: e.tensor_tensor(out=tB_[:, 0:n], in0=pas[2], in1=tB_[:, 0:n], op=ALU.mult), [t5, toks[2]])
        t7 = P.op("act", lambda e: e.activation(out=tC[:, 0:n], in_=pas[3], func=AF.Silu), [toks[3], stc["fc"]])
        t8 = P.op("dve", lambda e: e.tensor_tensor(out=B[:, j, t0 - HL:t0 - HL + n], in0=tB_[:, 0:n], in1=tC[:, 0:n], op=ALU.mult), [t6, t7])
        stc["fb"] = t8
        stc["fc"] = t8
        stc["last"] = t8
        if bi == 2 and ci == 1:
            stc["zfree"][g % 2] = t8
        return [t1, t2, t6, t7]

    kb.gemm([dict(W=w_in, K=D, col=lambda g: 2048 + g * 256, rhs=rhsA_ext, deps=tn),
             dict(W=w_in, K=D, col=lambda g: 4096 + g * 256, rhs=rhsA_ext, deps=tn),
             dict(W=w_in, K=D, col=lambda g: g * 256, rhs=rhsA_ext, deps=tn),
             dict(W=w_in, K=D, col=lambda g: 6144 + g * 256, rhs=rhsA_ext, deps=tn)], 8, TBE, epi_conv)

    tres = kb.outproj_residual(w_out, X, B, TB2, a_deps=stc["last"], x_deps=[tx])
    tple = kb.ple(w_pp, w_pg, 1, pn, X, A, pT, TB2, x_deps=[tres, tpn], a_free=[tres], p_deps=[tp])
    kb.out_toks.append(kb.load(xout.ap(), X[:], deps=[tple]))

    tn2 = kb.rmsnorm(lambda c, t0, n: X[:, c, t0:t0 + n], ng2, lambda c, t0, n: A[:, c, t0:t0 + n], TB2, deps=[tple, tng2], out_deps=[tple])
    sts = {"rr": 0, "free": [None] * 8}

    def epi_front(g, ci, bi, pas, toks):
        j = g * 2 + ci
        t0, n = TB2[bi]
        rel = []
        for si, (func, dst) in enumerate([(AF.Copy, uout), (AF.Silu, sout)]):
            if bi == 0:
                i = sts["rr"]
                sts["rr"] = (i + 1) % 8
                sts[("slot", si)] = i
            i = sts[("slot", si)]
            S = ST[i]
            t1 = P.op("act", lambda e, S=S, si=si, func=func: e.activation(out=S[:, t0:t0 + n], in_=pas[si], func=func),
                      [toks[si], sts["free"][i], tres])
            rel.append(t1)
            if bi == 1:
                td = kb.load(dst.ap()[:, j, :], S, deps=[t1], key=f"st{i}")
                sts["free"][i] = td
                kb.out_toks.append(td)
        return rel

    rhsA = lambda kc, t0, n: A[:, kc, t0:t0 + n]
    kb.gemm([dict(W=w2_in, K=D, col=lambda g: g * 256, rhs=rhsA, deps=[tn2]),
             dict(W=w2_in, K=D, col=lambda g: 2048 + g * 256, rhs=rhsA, deps=[tn2])], 8, TB2, epi_front)
    return kb.finish()


def tok_maps(arrs_fn, ncore):
    return [arrs_fn(c // 4, c % 4) for c in range(ncore)]


def run_l1(inp, x1, ncore=NCORE):
    p = np.asarray(inp["p"], np.float32)
    common = dict(w_in=np.asarray(inp["conv_w_in"][0]), w_out=np.asarray(inp["conv_w_out"][0]),
                  ple_proj=np.asarray(inp["ple_proj"][1]), ple_gate=np.asarray(inp["ple_gate"][1]),
                  norm_g=vec_fm(np.asarray(inp["norm_g"][1])), ple_norm=vec_fm(np.asarray(inp["ple_norm"][1])),
                  convk=np.ascontiguousarray(np.asarray(inp["conv_kernel"][0]).T.reshape(DC, 128, 3).transpose(1, 0, 2)),
                  w2_in=np.asarray(inp["ssm_w_in"][0]), norm_g2=vec_fm(np.asarray(inp["norm_g"][2])))
    maps = []
    for c in range(ncore):
        b, t = c // 4, c % 4
        xe = np.zeros((T + 2, D), np.float32)
        xe[2:] = x1[b, t * T:(t + 1) * T]
        if t > 0:
            xe[:2] = x1[b, t * T - 2:t * T]
        m = dict(common)
        m["xT"] = fm(xe)
        m["pT"] = fm(p[1, b, t * T:(t + 1) * T])
        maps.append(m)
    res = run_bass_kernel_spmd(get_prog("l1", build_l1), maps, core_ids=list(range(ncore)))
    x2 = np.zeros_like(x1)
    u = np.zeros_like(x1)
    sg = np.zeros_like(x1)
    for c in range(ncore):
        b, t = c // 4, c % 4
        x2[b, t * T:(t + 1) * T] = unfm(np.asarray(res.results[c]["xo"]))
        u[b, t * T:(t + 1) * T] = unfm(np.asarray(res.results[c]["uo"]))
        sg[b, t * T:(t + 1) * T] = unfm(np.asarray(res.results[c]["so"]))
    return x2, u, sg


class Rot:
    def __init__(self, tiles):
        self.tiles = tiles
        self.free = [None] * len(tiles)
        self.i = 0

    def get(self):
        i = self.i
        self.i = (i + 1) % len(self.tiles)
        return self.tiles[i], self.free[i], i

    def done(self, i, tok):
        self.free[i] = tok


NPAIR = 16
SB = 512


def build_scan():
    kb = KB(nslots=1, wcols=128)
    P = kb.P
    u_d = P.dram("u", [NPAIR, 32, SEQ], F32, kind="ExternalInput")
    lrc_d = P.dram("lr_c", [128, NPAIR], F32, kind="ExternalInput")
    lic_d = P.dram("li_c", [128, NPAIR], F32, kind="ExternalInput")
    ldc_d = P.dram("ld_c", [128, NPAIR], F32, kind="ExternalInput")
    lrr_d = P.dram("lr_r", [32, NPAIR * 128], F32, kind="ExternalInput")
    lir_d = P.dram("li_r", [32, NPAIR * 128], F32, kind="ExternalInput")
    ldr_d = P.dram("ld_r", [32, NPAIR * 128], F32, kind="ExternalInput")
    bre_d = P.dram("b_re", [32, NPAIR * 128], F32, kind="ExternalInput")
    bim_d = P.dram("b_im", [32, NPAIR * 128], F32, kind="ExternalInput")
    cre_d = P.dram("c_re", [128, NPAIR * 32], F32, kind="ExternalInput")
    cim_d = P.dram("c_im", [128, NPAIR * 32], F32, kind="ExternalInput")
    dsk_d = P.dram("dskin", [32, NPAIR], F32, kind="ExternalInput")
    iot_d = P.dram("iotain", [128, SB + 1], F32, kind="ExternalInput")
    y_d = P.dram("y", [NPAIR, 32, SEQ], F32, kind="ExternalOutput")

    TWO_PI = 2.0 * math.pi

    def sincos_into(x, n, rows, scratch, sn, cs, deps):
        ki, kf, s2, c2 = scratch
        t = P.op("dve", lambda e: e.tensor_scalar(out=ki[0:rows, 0:n], in0=x, scalar1=1.0 / TWO_PI, scalar2=None, op0=ALU.mult), deps)
        t = P.op("dve", lambda e: e.tensor_copy(out=kf[0:rows, 0:n], in_=ki[0:rows, 0:n]), [t])
        t = P.op("dve", lambda e: e.scalar_tensor_tensor(out=kf[0:rows, 0:n], in0=kf[0:rows, 0:n], scalar=-TWO_PI, in1=x, op0=ALU.mult, op1=ALU.add), [t])
        ta = P.op("act", lambda e: e.activation(out=s2[0:rows, 0:n], in_=kf[0:rows, 0:n], func=AF.Sin, scale=0.5), [t])
        tb = P.op("act", lambda e: e.activation(out=c2[0:rows, 0:n], in_=kf[0:rows, 0:n], func=AF.Sin, scale=0.25), [t])
        tb = P.op("dve", lambda e: e.tensor_tensor(out=c2[0:rows, 0:n], in0=c2[0:rows, 0:n], in1=c2[0:rows, 0:n], op=ALU.mult), [tb])
        tb = P.op("dve", lambda e: e.tensor_scalar(out=c2[0:rows, 0:n], in0=c2[0:rows, 0:n], scalar1=-2.0, scalar2=1.0, op0=ALU.mult, op1=ALU.add), [tb])
        t1 = P.op("dve", lambda e: e.scalar_tensor_tensor(out=sn[0:rows, 0:n], in0=s2[0:rows, 0:n], scalar=2.0, in1=c2[0:rows, 0:n], op0=ALU.mult, op1=ALU.mult), [ta, tb])
        t2 = P.op("dve", lambda e: e.tensor_tensor(out=cs[0:rows, 0:n], in0=s2[0:rows, 0:n], in1=s2[0:rows, 0:n], op=ALU.mult), [t1])
        t3 = P.op("dve", lambda e: e.tensor_scalar(out=cs[0:rows, 0:n], in0=cs[0:rows, 0:n], scalar1=-2.0, scalar2=1.0, op0=ALU.mult, op1=ALU.add), [t2])
        return sn, cs, t3

    NT = SB + 1
    scratch = (P.sb("tb_ki", [128, NT], I32), P.sb("tb_kf", [128, NT], F32), P.sb("tb_s2", [128, NT], F32), P.sb("tb_c2", [128, NT], F32))
    pm = {k: P.sb("pm_" + k, [128, NT], F32) for k in ["lr", "li", "dt", "sn", "cs", "abr", "abi", "den", "tmp"]}

    def param_math(rows, n, lr_ap, li_ap, ld_ap, out_r, out_th, out_cre, out_cim, deps):
        v = lambda k: pm[k][0:rows, 0:n]
        lr, li, dt = v("lr"), v("li"), v("dt")
        t1 = kb.load(lr, lr_ap, deps=deps)
        t2 = kb.load(li, li_ap, deps=deps)
        t3 = kb.load(dt, ld_ap, deps=deps)
        t = P.op("act", lambda e: e.activation(out=dt, in_=dt, func=AF.Exp), [t3])
        t = P.op("dve", lambda e: e.tensor_tensor(out=out_r, in0=lr, in1=dt, op=ALU.mult), [t, t1, deps])
        tth = P.op("dve", lambda e: e.tensor_tensor(out=out_th, in0=li, in1=dt, op=ALU.mult), [t, t2])
        tr = P.op("act", lambda e: e.activation(out=out_r, in_=out_r, func=AF.Exp), [t])
        sn, cs, ts = sincos_into(out_th, n, rows, scratch, pm["sn"], pm["cs"], [tth])
        sn, cs = sn[0:rows, 0:n], cs[0:rows, 0:n]
        abr, abi, den, tmp = v("abr"), v("abi"), v("den"), v("tmp")
        cr_, ci_ = out_cre, out_cim
        t = P.op("dve", lambda e: e.tensor_tensor(out=abr, in0=out_r, in1=cs, op=ALU.mult), [tr, ts])
        t = P.op("dve", lambda e: e.tensor_tensor(out=abi, in0=out_r, in1=sn, op=ALU.mult), [t])
        t = P.op("dve", lambda e: e.tensor_scalar(out=abr, in0=abr, scalar1=-1.0, scalar2=None, op0=ALU.add), [t])
        t = P.op("dve", lambda e: e.tensor_tensor(out=den, in0=lr, in1=lr, op=ALU.mult), [t])
        t = P.op("dve", lambda e: e.tensor_tensor(out=tmp, in0=li, in1=li, op=ALU.mult), [t])
        t = P.op("dve", lambda e: e.tensor_tensor(out=den, in0=den, in1=tmp, op=ALU.add), [t])
        t = P.op("dve", lambda e: e.reciprocal(out=den, in_=den), [t])
        t = P.op("dve", lambda e: e.tensor_tensor(out=cr_, in0=abr, in1=lr, op=ALU.mult), [t])
        t = P.op("dve", lambda e: e.tensor_tensor(out=tmp, in0=abi, in1=li, op=ALU.mult), [t])
        t = P.op("dve", lambda e: e.tensor_tensor(out=cr_, in0=cr_, in1=tmp, op=ALU.add), [t])
        t = P.op("dve", lambda e: e.tensor_tensor(out=cr_, in0=cr_, in1=den, op=ALU.mult), [t])
        t = P.op("dve", lambda e: e.tensor_tensor(out=ci_, in0=abi, in1=lr, op=ALU.mult), [t])
        t = P.op("dve", lambda e: e.tensor_tensor(out=tmp, in0=abr, in1=li, op=ALU.mult), [t])
        t = P.op("dve", lambda e: e.tensor_tensor(out=ci_, in0=ci_, in1=tmp, op=ALU.subtract), [t])
        t = P.op("dve", lambda e: e.tensor_tensor(out=ci_, in0=ci_, in1=den, op=ALU.mult), [t])
        return t

    pc = {k: P.sb("pc_" + k, [128, NPAIR], F32) for k in ["r", "th", "cre", "cim"]}
    tpc = param_math(128, NPAIR, lrc_d.ap(), lic_d.ap(), ldc_d.ap(), pc["r"][:], pc["th"][:], pc["cre"][:], pc["cim"][:], [])
    pr = {k: P.sb("pr_" + k, [32, NPAIR * 128], F32) for k in ["cre", "cim"]}
    prj = {k: P.sb("prj_" + k, [32, 512], F32) for k in ["r", "th"]}
    tpr = tpc
    for q in range(4):
        cs_ = slice(q * 512, (q + 1) * 512)
        tpr = param_math(32, 512, lrr_d.ap()[:, cs_], lir_d.ap()[:, cs_], ldr_d.ap()[:, cs_], prj["r"][:], prj["th"][:],
                         pr["cre"][:, cs_], pr["cim"][:, cs_], [tpr])

    Bre = P.sb("Bre", [32, NPAIR * 128], F32)
    Bim = P.sb("Bim", [32, NPAIR * 128], F32)
    bbr = P.sb("bbr", [32, NPAIR * 128], BF16)
    bbi = P.sb("bbi", [32, NPAIR * 128], BF16)
    tb1 = kb.load(Bre[:], bre_d.ap())
    tb2 = kb.load(Bim[:], bim_d.ap())
    tmp = P.sb("bb_tmp", [32, NPAIR * 128], F32)
    tmp2 = P.sb("bb_tmp2", [32, NPAIR * 128], F32)
    t = P.op("dve", lambda e: e.tensor_tensor(out=tmp[:], in0=Bre[:], in1=pr["cre"][:], op=ALU.mult), [tpr, tb1])
    t = P.op("dve", lambda e: e.tensor_tensor(out=tmp2[:], in0=Bim[:], in1=pr["cim"][:], op=ALU.mult), [t, tb2])
    t = P.op("dve", lambda e: e.tensor_tensor(out=bbr[:], in0=tmp[:], in1=tmp2[:], op=ALU.subtract), [t])
    t = P.op("dve", lambda e: e.tensor_tensor(out=tmp[:], in0=Bim[:], in1=pr["cre"][:], op=ALU.mult), [t])
    t = P.op("dve", lambda e: e.tensor_tensor(out=tmp2[:], in0=Bre[:], in1=pr["cim"][:], op=ALU.mult), [t])
    tbb = P.op("dve", lambda e: e.tensor_tensor(out=bbi[:], in0=tmp[:], in1=tmp2[:], op=ALU.add), [t])

    Cf = P.sb("Cf", [128, NPAIR * 32], F32)
    Cf2 = P.sb("Cf2", [128, NPAIR * 32], F32)
    Cre = P.sb("Cre", [128, NPAIR * 32], BF16)
    nCim = P.sb("nCim", [128, NPAIR * 32], BF16)
    tc1 = kb.load(Cf[:], cre_d.ap())
    tc2 = kb.load(Cf2[:], cim_d.ap())
    tcc = P.op("act", lambda e: e.activation(out=Cre[:], in_=Cf[:], func=AF.Copy), [tc1])
    tcc2 = P.op("act", lambda e: e.activation(out=nCim[:], in_=Cf2[:], func=AF.Copy, scale=-1.0), [tc2])
    dsk = P.sb("dsk", [32, NPAIR], F32)
    tdk = kb.load(dsk[:], dsk_d.ap())
    iot = P.sb("iot", [128, SB + 1], F32)
    tio = kb.load(iot[:], iot_d.ap())

    ang = P.sb("ang", [128, NT], F32)
    Ctr = Rot([P.sb(f"Ct{i}", [128, NT], F32) for i in range(2)])
    Str = Rot([P.sb(f"St{i}", [128, NT], F32) for i in range(2)])
    Ur = Rot([P.sb(f"U{i}", [32, SEQ], F32) for i in range(1)])
    Ubr = Rot([P.sb(f"Ub{i}", [32, SEQ], BF16) for i in range(2)])
    BRr = Rot([P.sb(f"BR{i}", [128, SB], F32) for i in range(2)])
    BIr = Rot([P.sb(f"BI{i}", [128, SB], F32) for i in range(2)])
    MRr = Rot([P.sb(f"MR{i}", [128, SB], F32) for i in range(2)])
    MIr = Rot([P.sb(f"MI{i}", [128, SB], F32) for i in range(2)])
    T1r = Rot([P.sb(f"T1{i}", [128, SB], F32) for i in range(2)])
    T2r = Rot([P.sb(f"T2{i}", [128, SB], F32) for i in range(2)])
    WRr = Rot([P.sb(f"WR{i}", [128, SB], F32) for i in range(2)])
    WIr = Rot([P.sb(f"WI{i}", [128, SB], F32) for i in range(2)])
    SRr = Rot([P.sb(f"SR{i}", [128, SB], BF16) for i in range(2)])
    SIr = Rot([P.sb(f"SI{i}", [128, SB], BF16) for i in range(2)])
    YOr = Rot([P.sb(f"YO{i}", [32, SB], F32) for i in range(2)])
    car = [P.sb(f"car{i}", [128, 2], F32) for i in range(2)]
    ctmp = P.sb("ctmp", [128, 2], F32)
    tz = P.op("dve", lambda e: e.memset(car[0][:], 0.0))
    car_tok = [tz, None]
    ang_free = None
    NBLK = SEQ // SB
    for p in range(NPAIR):
        U, uf, ui = Ur.get()
        tu = kb.load(U[:], u_d.ap()[p], deps=[uf], key=f"u{ui}")
        Ub, ubf, ubi = Ubr.get()
        tub = P.dma("pool", f"ub{ubi}", Ub[:], u_d.ap()[p], deps=[ubf])
        Ct, cf, cti = Ctr.get()
        St, sf, sti = Str.get()
        ta_ = P.op("dve", lambda e, p=p: e.tensor_scalar(out=ang[:], in0=iot[:], scalar1=pc["th"][:, p:p + 1], scalar2=None, op0=ALU.mult),
                   [tio, tpc, ang_free])
        _, _, ttab = sincos_into(ang[:], NT, 128, scratch, St, Ct, [ta_, cf, sf, tpr])
        ang_free = ttab
        rcol = pc["r"][:, p:p + 1]
        last_pe_u = None
        last_dve_u = None
        last_ct = None
        for k in range(NBLK):
            c0 = k * SB
            b1, bd1 = kb.bank()
            b2, bd2 = kb.bank()
            m1 = P.op("pe", lambda e, b1=b1, Ub=Ub, p=p, c0=c0: e.matmul(kb.psum[:, b1, :], bbr[:, p * 128:(p + 1) * 128], Ub[:, c0:c0 + SB], start=True, stop=True),
                      [bd1, tbb, tub])
            m2 = P.op("pe", lambda e, b2=b2, Ub=Ub, p=p, c0=c0: e.matmul(kb.psum[:, b2, :], bbi[:, p * 128:(p + 1) * 128], Ub[:, c0:c0 + SB], start=True, stop=True),
                      [bd2])
            last_pe_u = m2
            BR, brf, bri = BRr.get()
            BI, bif, bii = BIr.get()
            e1 = P.op("act", lambda e, BR=BR, b1=b1: e.activation(out=BR[:], in_=kb.psum[:, b1, :], func=AF.Copy), [m1, brf])
            e2 = P.op("act", lambda e, BI=BI, b2=b2: e.activation(out=BI[:], in_=kb.psum[:, b2, :], func=AF.Copy), [m2, bif])
            kb.bank_tok[b1] = e1
            kb.bank_tok[b2] = e2
            C_, S_ = Ct[:, 0:SB], St[:, 0:SB]
            MR, mrf, mri = MRr.get()
            MI, mif, mii = MIr.get()
            T1, t1f, t1i = T1r.get()
            T2, t2f, t2i = T2r.get()
            g1 = P.op("pool", lambda e, MR=MR, BR=BR, C_=C_: e.tensor_tensor(out=MR[:], in0=BR[:], in1=C_, op=ALU.mult), [e1, ttab, mrf])
            g2 = P.op("pool", lambda e, T1=T1, BI=BI, S_=S_: e.tensor_tensor(out=T1[:], in0=BI[:], in1=S_, op=ALU.mult), [e2, t1f])
            g3 = P.op("pool", lambda e, MR=MR, T1=T1: e.tensor_tensor(out=MR[:], in0=MR[:], in1=T1[:], op=ALU.add), [g2])
            d1 = P.op("dve", lambda e, MI=MI, BI=BI, C_=C_: e.tensor_tensor(out=MI[:], in0=BI[:], in1=C_, op=ALU.mult), [e2, ttab, mif])
            d2 = P.op("dve", lambda e, T2=T2, BR=BR, S_=S_: e.tensor_tensor(out=T2[:], in0=BR[:], in1=S_, op=ALU.mult), [e1, t2f])
            d3 = P.op("dve", lambda e, MI=MI, T2=T2: e.tensor_tensor(out=MI[:], in0=MI[:], in1=T2[:], op=ALU.subtract), [d2])
            BRr.done(bri, [g1, d2])
            BIr.done(bii, [g2, d1])
            T1r.done(t1i, g3)
            T2r.done(t2i, d3)
            WR, wrf, wri = WRr.get()
            WI, wif, wii = WIr.get()
            cin = car[k % 2] if k > 0 else car[0]
            cdep = car_tok[k % 2] if k > 0 else tz
            if k == 0:
                s1 = P.op("dve", lambda e, WR=WR, MR=MR, rcol=rcol: e.tensor_tensor_scan(out=WR[:], data0=rcol.to_broadcast([128, SB]), data1=MR[:], initial=0.0, op0=ALU.mult, op1=ALU.add),
                          [g3, wrf, tpc])
                s2_ = P.op("dve", lambda e, WI=WI, MI=MI, rcol=rcol: e.tensor_tensor_scan(out=WI[:], data0=rcol.to_broadcast([128, SB]), data1=MI[:], initial=0.0, op0=ALU.mult, op1=ALU.add),
                           [d3, wif])
            else:
                s1 = P.op("dve", lambda e, WR=WR, MR=MR, rcol=rcol, cin=cin: e.tensor_tensor_scan(out=WR[:], data0=rcol.to_broadcast([128, SB]), data1=MR[:], initial=cin[:, 0:1], op0=ALU.mult, op1=ALU.add),
                          [g3, wrf, cdep])
                s2_ = P.op("dve", lambda e, WI=WI, MI=MI, rcol=rcol, cin=cin: e.tensor_tensor_scan(out=WI[:], data0=rcol.to_broadcast([128, SB]), data1=MI[:], initial=cin[:, 1:2], op0=ALU.mult, op1=ALU.add),
                           [d3, wif])
            MRr.done(mri, s1)
            MIr.done(mii, s2_)
            if k < NBLK - 1:
                cn = car[(k + 1) % 2]
                C5, S5 = Ct[:, SB:SB + 1], St[:, SB:SB + 1]
                q1 = P.op("dve", lambda e, WI=WI, S5=S5: e.tensor_scalar(out=ctmp[:, 0:1], in0=WI[:, SB - 1:SB], scalar1=S5, scalar2=None, op0=ALU.mult), [s2_, s1])
                q2 = P.op("dve", lambda e, WR=WR, C5=C5, cn=cn: e.scalar_tensor_tensor(out=cn[:, 0:1], in0=WR[:, SB - 1:SB], scalar=C5, in1=ctmp[:, 0:1], op0=ALU.mult, op1=ALU.subtract), [q1])
                q3 = P.op("dve", lambda e, WI=WI, C5=C5: e.tensor_scalar(out=ctmp[:, 1:2], in0=WI[:, SB - 1:SB], scalar1=C5, scalar2=None, op0=ALU.mult), [q2])
                q4 = P.op("dve", lambda e, WR=WR, S5=S5, cn=cn: e.scalar_tensor_tensor(out=cn[:, 1:2], in0=WR[:, SB - 1:SB], scalar=S5, in1=ctmp[:, 1:2], op0=ALU.mult, op1=ALU.add), [q3])
                car_tok[(k + 1) % 2] = q4
            SR, srf, sri = SRr.get()
            SI, sif, sii = SIr.get()
            T1, t1f, t1i = T1r.get()
            T2, t2f, t2i = T2r.get()
            MR2, mrf2, mri2 = MRr.get()
            MI2, mif2, mii2 = MIr.get()
            h1 = P.op("pool", lambda e, MR2=MR2, WR=WR, C_=C_: e.tensor_tensor(out=MR2[:], in0=WR[:], in1=C_, op=ALU.mult), [s1, mrf2])
            h2 = P.op("pool", lambda e, T1=T1, WI=WI, S_=S_: e.tensor_tensor(out=T1[:], in0=WI[:], in1=S_, op=ALU.mult), [s2_, t1f])
            h3 = P.op("pool", lambda e, SR=SR, MR2=MR2, T1=T1: e.tensor_tensor(out=SR[:], in0=MR2[:], in1=T1[:], op=ALU.subtract), [h2, srf])
            f1 = P.op("dve", lambda e, MI2=MI2, WR=WR, S_=S_: e.tensor_tensor(out=MI2[:], in0=WR[:], in1=S_, op=ALU.mult), [s1, mif2])
            f2 = P.op("dve", lambda e, T2=T2, WI=WI, C_=C_: e.tensor_tensor(out=T2[:], in0=WI[:], in1=C_, op=ALU.mult), [s2_, t2f])
            f3 = P.op("dve", lambda e, SI=SI, MI2=MI2, T2=T2: e.tensor_tensor(out=SI[:], in0=MI2[:], in1=T2[:], op=ALU.add), [f2, sif])
            last_ct = [h3, f3]
            MRr.done(mri2, h3)
            MIr.done(mii2, f3)
            T1r.done(t1i, h3)
            T2r.done(t2i, f3)
            wdeps = [h1, h2, f1, f2]
            if k < NBLK - 1:
                wdeps.append(q4)
            WRr.done(wri, wdeps)
            WIr.done(wii, wdeps)
            b3, bd3 = kb.bank()
            m3 = P.op("pe", lambda e, b3=b3, SR=SR, p=p: e.matmul(kb.psum[0:32, b3, :], Cre[:, p * 32:(p + 1) * 32], SR[:], start=True, stop=False), [bd3, h3, tcc])
            m4 = P.op("pe", lambda e, b3=b3, SI=SI, p=p: e.matmul(kb.psum[0:32, b3, :], nCim[:, p * 32:(p + 1) * 32], SI[:], start=False, stop=True), [f3, tcc2])
            SRr.done(sri, m3)
            SIr.done(sii, m4)
            YO, yof, yoi = YOr.get()
            o1 = P.op("dve", lambda e, YO=YO, U=U, b3=b3, p=p, c0=c0: e.scalar_tensor_tensor(out=YO[:], in0=U[:, c0:c0 + SB], scalar=dsk[:, p:p + 1], in1=kb.psum[0:32, b3, :],
                                                                                         op0=ALU.mult, op1=ALU.add), [m4, tu, tdk, yof])
            kb.bank_tok[b3] = o1
            last_dve_u = o1
            td = kb.load(y_d.ap()[p][:, c0:c0 + SB], YO[:], deps=[o1], key=f"yo{yoi}")
            YOr.done(yoi, td)
            kb.out_toks.append(td)
        Ur.done(ui, last_dve_u)
        Ubr.done(ubi, last_pe_u)
        Ctr.done(cti, last_ct)
        Str.done(sti, last_ct)
    return kb.finish()


def scan_params(inp, b, gq):
    g0 = gq * 32
    f = lambda k: np.asarray(inp[k][0], np.float32)
    lam_re, lam_im, log_dt = f("ssm_lam_re"), f("ssm_lam_im"), f("ssm_log_dt")
    b_re, b_im, c_re, c_im, dsk = f("ssm_b_re"), f("ssm_b_im"), f("ssm_c_re"), f("ssm_c_im"), f("ssm_d")
    G = slice(g0, g0 + 32)
    col = lambda a: np.ascontiguousarray(a[G].reshape(NPAIR, 128).T)
    ld_full = np.broadcast_to(log_dt[:, None], (128, 64))
    row = lambda a: np.ascontiguousarray(np.broadcast_to(a[G].reshape(1, NPAIR * 128), (32, NPAIR * 128)))
    bT = np.zeros((32, NPAIR, 128), np.float32)
    bTi = np.zeros((32, NPAIR, 128), np.float32)
    cbd = np.zeros((128, NPAIR, 32), np.float32)
    cbdi = np.zeros((128, NPAIR, 32), np.float32)
    for p in range(NPAIR):
        for gl in range(2):
            g = g0 + 2 * p + gl
            bT[gl * 16:(gl + 1) * 16, p, gl * 64:(gl + 1) * 64] = b_re[g].T
            bTi[gl * 16:(gl + 1) * 16, p, gl * 64:(gl + 1) * 64] = b_im[g].T
            cbd[gl * 64:(gl + 1) * 64, p, gl * 16:(gl + 1) * 16] = c_re[g].T
            cbdi[gl * 64:(gl + 1) * 64, p, gl * 16:(gl + 1) * 16] = c_im[g].T
    return dict(lr_c=col(lam_re), li_c=col(lam_im), ld_c=col(ld_full), lr_r=row(lam_re), li_r=row(lam_im), ld_r=row(ld_full),
                b_re=bT.reshape(32, -1), b_im=bTi.reshape(32, -1), c_re=cbd.reshape(128, -1), c_im=cbdi.reshape(128, -1),
                dskin=np.ascontiguousarray(dsk.reshape(128, 16)[G].reshape(NPAIR, 32).T),
                iotain=np.ascontiguousarray(np.broadcast_to(np.arange(SB + 1, dtype=np.float32)[None, :], (128, SB + 1))))


def run_scan(inp, u, ncore=NCORE):
    maps = []
    for c in range(ncore):
        b, gq = c // 4, c % 4
        m = scan_params(inp, b, gq)
        uc = u[b][:, gq * 512:(gq + 1) * 512]
        m["u"] = np.ascontiguousarray(uc.T.reshape(NPAIR, 32, SEQ))
        maps.append(m)
    res = run_bass_kernel_spmd(get_prog("scan", build_scan), maps, core_ids=list(range(ncore)))
    y = np.zeros_like(u)
    for c in range(ncore):
        b, gq = c // 4, c % 4
        y[b][:, gq * 512:(gq + 1) * 512] = np.asarray(res.results[c]["y"]).reshape(512, SEQ).T
    return y


GELU_C = 2.0 * math.sqrt(2.0 / math.pi)


def build_l2b():
    kb = KB(nslots=4)
    P = kb.P
    xin = P.dram("xT", [128, DC, T], F32, kind="ExternalInput")
    yin = P.dram("yT", [128, DC, T], F32, kind="ExternalInput")
    sgin = P.dram("sgT", [128, DC, T], F32, kind="ExternalInput")
    pin = P.dram("pT", [128, 2, T], F32, kind="ExternalInput")
    w_glu = P.dram("w_glu", [D, 4096], F32, kind="ExternalInput")
    bgl_d = P.dram("b_glu", [128, 32], F32, kind="ExternalInput")
    w_out = P.dram("w_out", [D, D], F32, kind="ExternalInput")
    w_pp = P.dram("ple_proj", [256, D], F32, kind="ExternalInput")
    w_pg = P.dram("ple_gate", [D, D], F32, kind="ExternalInput")
    pn_d = P.dram("ple_norm", [128, DC], F32, kind="ExternalInput")
    ng3_d = P.dram("norm_g3", [128, DC], F32, kind="ExternalInput")
    w3_in = P.dram("w3_in", [D, 8192], F32, kind="ExternalInput")
    w_fg = P.dram("w_fg", [D, 32], F32, kind="ExternalInput")
    nbfg_d = P.dram("b_fg", [32, 1], F32, kind="ExternalInput")
    xout = P.dram("xo", [128, DC, T], F32, kind="ExternalOutput")
    qout = P.dram("qo", [128, DC, T], BF16, kind="ExternalOutput")
    kout = P.dram("ko", [128, DC, T], BF16, kind="ExternalOutput")
    vout = P.dram("vo", [128, DC, T], BF16, kind="ExternalOutput")
    gout = P.dram("go", [128, DC, T], F32, kind="ExternalOutput")
    lfout = P.dram("lfo", [32, T], F32, kind="ExternalOutput")

    X = P.sb("X", [128, DC, T], F32)
    A = P.sb("A", [128, DC, T], BF16)
    Braw = P.sb("Braw", [128, 8192], F32)
    B = Braw[:].bitcast(BF16).rearrange("p (c t) -> p c t", c=DC)
    ST = [Braw[:, i * 1024:(i + 1) * 1024] for i in range(8)]
    pn = P.sb("pn", [128, DC], F32)
    ng3 = P.sb("ng3", [128, DC], F32)
    bgl = P.sb("bgl", [128, 32], F32)
    bfg = P.sb("bfg", [32, 1], F32)
    pT = P.sb("pTb", [128, 2, T], BF16)
    tx = kb.load(X[:], xin.ap())
    tpn = kb.load(pn[:], pn_d.ap())
    tng3 = kb.load(ng3[:], ng3_d.ap())
    tbg = kb.load(bgl[:], bgl_d.ap())
    tbf = kb.load(bfg[:], nbfg_d.ap())
    tp = P.dma("pool", "pcast", pT[:], pin.ap())
    nbf = P.sb("nbf", [32, 1], F32)
    tnbf = P.op("dve", lambda e: e.tensor_scalar(out=nbf[:], in0=bfg[:], scalar1=-1.0, scalar2=None, op0=ALU.mult), [tbf])

    Yr = Rot([P.sb(f"Y{i}", [128, 512], F32) for i in range(2)])
    G1 = Rot([P.sb(f"G1{i}", [128, 512], F32) for i in range(2)])
    tg = None
    for c in range(DC):
        for (h0, hn) in TB2:
            Y, yf, yi = Yr.get()
            ty = kb.load(Y[:], yin.ap()[:, c, h0:h0 + hn], deps=[yf], key=f"y{yi}")
            G, gf, gi = G1.get()
            t = P.op("dve", lambda e, G=G, Y=Y: e.tensor_tensor(out=G[:], in0=Y[:], in1=Y[:], op=ALU.mult), [ty, gf])
            t = P.op("dve", lambda e, G=G: e.tensor_scalar(out=G[:], in0=G[:], scalar1=0.044715, scalar2=1.0, op0=ALU.mult, op1=ALU.add), [t])
            t = P.op("dve", lambda e, G=G, Y=Y: e.tensor_tensor(out=G[:], in0=G[:], in1=Y[:], op=ALU.mult), [t])
            t = P.op("act", lambda e, G=G: e.activation(out=G[:], in_=G[:], func=AF.Sigmoid, scale=GELU_C), [t])
            tg = P.op("dve", lambda e, G=G, Y=Y, c=c, h0=h0, hn=hn: e.tensor_tensor(out=A[:, c, h0:h0 + hn], in0=G[:], in1=Y[:], op=ALU.mult), [t])
            Yr.done(yi, tg)
            G1.done(gi, tg)

    SGr = Rot([P.sb(f"SG{i}", [128, 512], F32) for i in range(3)])
    tA, tB_ = kb.scrv(0), kb.scrv(1)
    stg = {"fa": None, "fb": None, "last": None, "sg": {}}

    def sg_load(j, bi):
        if j < DC and (j, bi) not in stg["sg"]:
            S, sf, si = SGr.get()
            t0, n = TB2[bi]
            t = kb.load(S[:], sgin.ap()[:, j, t0:t0 + n], deps=[sf], key=f"sg{si}")
            stg["sg"][(j, bi)] = (S, si, t)

    sg_load(0, 0)

    def epi_glu(g, ci, bi, pas, toks):
        j = g * 2 + ci
        t0, n = TB2[bi]
        sg_load(j, bi)
        nj, nb = (j, 1) if bi == 0 else (j + 1, 0)
        sg_load(nj, nb)
        S, si, ts = stg["sg"][(j, bi)]
        t1 = P.op("act", lambda e: e.activation(out=tA[:, 0:n], in_=pas[1], func=AF.Sigmoid, bias=bgl[:, 16 + j:17 + j]), [toks[1], stg["fa"], tbg])
        t2 = P.op("dve", lambda e: e.scalar_tensor_tensor(out=tB_[:, 0:n], in0=pas[0], scalar=bgl[:, j:j + 1], in1=tA[:, 0:n], op0=ALU.add, op1=ALU.mult),
                  [t1, toks[0], stg["fb"]])
        t3 = P.op("dve", lambda e: e.tensor_tensor(out=B[:, j, t0:t0 + n], in0=tB_[:, 0:n], in1=S[:, 0:n], op=ALU.mult), [t2, ts])
        stg["fa"] = t2
        stg["fb"] = t3
        stg["last"] = t3
        SGr.done(si, t3)
        return [t2, t1]

    rhsA = lambda kc, t0, n: A[:, kc, t0:t0 + n]
    kb.gemm([dict(W=w_glu, K=D, col=lambda g: g * 256, rhs=rhsA, deps=[tg]),
             dict(W=w_glu, K=D, col=lambda g: 2048 + g * 256, rhs=rhsA, deps=[tg])], 8, TB2, epi_glu)

    tres = kb.outproj_residual(w_out, X, B, TB2, a_deps=stg["last"], x_deps=[tx])
    tple = kb.ple(w_pp, w_pg, 2, pn, X, A, pT, TB2, x_deps=[tres, tpn], a_free=[tres], p_deps=[tp])
    kb.out_toks.append(kb.load(xout.ap(), X[:], deps=[tple]))

    tn3 = kb.rmsnorm(lambda c, t0, n: X[:, c, t0:t0 + n], ng3, lambda c, t0, n: A[:, c, t0:t0 + n], TB2, deps=[tple, tng3], out_deps=[tple])
    sts = {"rr": 0, "free": [None] * 8}

    def make_epi(outs):
        def epi(g, ci, bi, pas, toks):
            j = g * 2 + ci
            t0, n = TB2[bi]
            rel = []
            for si, (func, scale, dst, isbf) in enumerate(outs):
                if bi == 0:
                    i = sts["rr"]
                    sts["rr"] = (i + 1) % 8
                    sts[("slot", si)] = i
                i = sts[("slot", si)]
                S = ST[i].bitcast(BF16)[:, 0:T] if isbf else ST[i]
                t1 = P.op("act", lambda e, S=S, si=si, func=func, scale=scale: e.activation(out=S[:, t0:t0 + n], in_=pas[si], func=func, scale=scale),
                          [toks[si], sts["free"][i], tres])
                rel.append(t1)
                if bi == 1:
                    td = kb.load(dst.ap()[:, j, :], S, deps=[t1], key=f"st{i}")
                    sts["free"][i] = td
                    kb.out_toks.append(td)
            return rel
        return epi

    kb.gemm([dict(W=w3_in, K=D, col=lambda g: g * 256, rhs=rhsA, deps=[tn3]),
             dict(W=w3_in, K=D, col=lambda g: 2048 + g * 256, rhs=rhsA, deps=[tn3])], 8, TB2,
            make_epi([(AF.Copy, 0.125, qout, True), (AF.Copy, 1.0, kout, True)]))
    kb.gemm([dict(W=w3_in, K=D, col=lambda g: 4096 + g * 256, rhs=rhsA, deps=[tn3]),
             dict(W=w3_in, K=D, col=lambda g: 6144 + g * 256, rhs=rhsA, deps=[tn3])], 8, TB2,
            make_epi([(AF.Copy, 1.0, vout, True), (AF.Silu, 1.0, gout, False)]))

    LF = P.sb("LF", [32, T], F32)

    def epi_fg(g, ci, bi, pas, toks):
        t0, n = TB2[bi]
        t1 = P.op("act", lambda e: e.activation(out=LF[:, t0:t0 + n], in_=pas[0], func=AF.Exp, scale=-1.0, bias=nbf[:, 0:1]), [toks[0], tnbf])
        t2 = P.op("act", lambda e: e.activation(out=LF[:, t0:t0 + n], in_=LF[:, t0:t0 + n], func=AF.Ln, bias=kb.cst[0:32, 2:3]), [t1, kb.t_cst])
        t3 = P.op("dve", lambda e: e.tensor_scalar(out=LF[:, t0:t0 + n], in0=LF[:, t0:t0 + n], scalar1=-1.0, scalar2=None, op0=ALU.mult), [t2])
        if bi == 1:
            kb.out_toks.append(kb.load(lfout.ap(), LF[:], deps=[t3]))
        return [t1]

    kb.gemm([dict(W=w_fg, K=D, col=lambda g: 0, rhs=rhsA, deps=[tn3])], 1, TB2, epi_fg, gcols=32)
    return kb.finish()


def run_l2b(inp, x2, y, sg, ncore=NCORE):
    p = np.asarray(inp["p"], np.float32)
    common = dict(w_glu=np.asarray(inp["ssm_w_glu"][0]), b_glu=vec_fm(np.asarray(inp["ssm_b_glu"][0])),
                  w_out=np.asarray(inp["ssm_w_out"][0]),
                  ple_proj=np.asarray(inp["ple_proj"][2]), ple_gate=np.asarray(inp["ple_gate"][2]),
                  ple_norm=vec_fm(np.asarray(inp["ple_norm"][2])), norm_g3=vec_fm(np.asarray(inp["norm_g"][3])),
                  w3_in=np.asarray(inp["fox_w_in"][0]), w_fg=np.asarray(inp["fox_w_fg"][0]),
                  b_fg=np.asarray(inp["fox_b_fg"][0], np.float32).reshape(32, 1))
    maps = []
    for c in range(ncore):
        b, t = c // 4, c % 4
        sl = slice(t * T, (t + 1) * T)
        m = dict(common)
        m["xT"] = fm(x2[b, sl]); m["yT"] = fm(y[b, sl]); m["sgT"] = fm(sg[b, sl]); m["pT"] = fm(p[2, b, sl])
        maps.append(m)
    res = run_bass_kernel_spmd(get_prog("l2b", build_l2b), maps, core_ids=list(range(ncore)))
    B_ = x2.shape[0]
    x3 = np.zeros_like(x2)
    q = np.zeros((B_, SEQ, D), NPBF); k = np.zeros((B_, SEQ, D), NPBF); v = np.zeros((B_, SEQ, D), NPBF)
    g = np.zeros_like(x2)
    lf = np.zeros((B_, SEQ, 32), np.float32)
    for c in range(ncore):
        b, t = c // 4, c % 4
        sl = slice(t * T, (t + 1) * T)
        r = res.results[c]
        x3[b, sl] = unfm(np.asarray(r["xo"])); g[b, sl] = unfm(np.asarray(r["go"]))
        q[b, sl] = unfm(np.asarray(r["qo"])); k[b, sl] = unfm(np.asarray(r["ko"])); v[b, sl] = unfm(np.asarray(r["vo"]))
        lf[b, sl] = np.asarray(r["lfo"]).T
    return x3, q, k, v, g, lf


HPC = 8


def build_fox():
    kb = KB(nslots=1, wcols=128)
    P = kb.P
    q_d = P.dram("q", [HPC, 64, SEQ], BF16, kind="ExternalInput")
    k_d = P.dram("k", [HPC, 64, SEQ], BF16, kind="ExternalInput")
    v_d = P.dram("v", [HPC, 128, 32, 64], BF16, kind="ExternalInput")
    lf_d = P.dram("lf", [HPC, SEQ], F32, kind="ExternalInput")
    dm_d = P.dram("dmask", [128, 4, 512], F32, kind="ExternalInput")
    o_d = P.dram("o", [HPC, 64, SEQ], F32, kind="ExternalOutput")

    LFt = P.sb("LFt", [HPC, SEQ], F32)
    Cc = P.sb("Cc", [HPC, SEQ], F32)
    R1 = P.sb("R1", [HPC, SEQ], F32)
    CH = P.sb("CH", [HPC, 3, SEQ], BF16)
    NCH = P.sb("NCH", [HPC, 3, SEQ], BF16)
    dm = P.sb("dm", [128, 4, 512], F32)
    tl = kb.load(LFt[:], lf_d.ap())
    tdm = kb.load(dm[:], dm_d.ap())
    t = P.op("dve", lambda e: e.tensor_tensor_scan(out=Cc[:], data0=kb.cst[0:HPC, 2:3].to_broadcast([HPC, SEQ]), data1=LFt[:], initial=0.0,
                                                  op0=ALU.mult, op1=ALU.add), [tl, kb.t_cst])
    t = P.op("dve", lambda e: e.tensor_copy(out=CH[:, 0, :], in_=Cc[:]), [t])
    t = P.op("dve", lambda e: e.tensor_tensor(out=R1[:], in0=Cc[:], in1=CH[:, 0, :], op=ALU.subtract), [t])
    t = P.op("dve", lambda e: e.tensor_copy(out=CH[:, 1, :], in_=R1[:]), [t])
    t = P.op("dve", lambda e: e.tensor_tensor(out=R1[:], in0=R1[:], in1=CH[:, 1, :], op=ALU.subtract), [t])
    t = P.op("dve", lambda e: e.tensor_copy(out=CH[:, 2, :], in_=R1[:]), [t])
    tch = P.op("dve", lambda e: e.tensor_scalar(out=NCH[:], in0=CH[:], scalar1=-1.0, scalar2=None, op0=ALU.mult), [t])

    QAr = Rot([P.sb(f"QA{i}", [70, SEQ], BF16) for i in range(2)])
    KAr = Rot([P.sb(f"KA{i}", [70, SEQ], BF16) for i in range(2)])
    VEr = Rot([P.sb(f"VE{i}", [128, 32, 128], BF16) for i in range(2)])
    init_tok = []
    for i in range(2):
        init_tok.append([
            P.op("dve", lambda e, i=i: e.memset(QAr.tiles[i][64:70, :], 1.0)),
            P.op("dve", lambda e, i=i: e.memset(KAr.tiles[i][64:70, :], 1.0)),
            P.op("dve", lambda e, i=i: e.memset(VEr.tiles[i][:], 1.0))])
    Er = Rot([P.sb(f"E{i}", [128, 512], BF16) for i in range(4)])
    Ef = Rot([P.sb(f"Ef{i}", [128, 512], F32) for i in range(2)])
    Rr = Rot([P.sb(f"Rc{i}", [64, 512], F32) for i in range(2)])
    Or = Rot([P.sb(f"Oo{i}", [64, 512], F32) for i in range(2)])
    import os
    FOX_NH = int(os.environ.get("FOX_NH", str(HPC)))
    FOX_NQ = int(os.environ.get("FOX_NQ", "8"))
    FOX_AUG = int(os.environ.get("FOX_AUG", "1"))
    FOX_PH = int(os.environ.get("FOX_PH", "9"))
    fox_rr = {"o": 0, "s": 0}
    for h in range(FOX_NH):
        QA, qf, qi = QAr.get()
        KA, kf, ki = KAr.get()
        VE, vf, vi = VEr.get()
        tq = [kb.load(QA[0:64, :], q_d.ap()[h], deps=[qf, init_tok[qi]], key=f"q{qi}")]
        tk = [kb.load(KA[0:64, :], k_d.ap()[h], deps=[kf, init_tok[ki]], key=f"k{ki}")]
        if FOX_AUG:
            tq.append(kb.load(QA[64:67, :], CH[h:h + 1, :, :], deps=[qf, tch, init_tok[qi]], key=f"qc{qi}"))
            tk.append(kb.load(KA[67:70, :], NCH[h:h + 1, :, :], deps=[kf, tch, init_tok[ki]], key=f"kc{ki}"))
        tv = kb.load(VE[:, :, 0:64], v_d.ap()[h], deps=[vf, init_tok[vi]], key=f"v{vi}")
        last_pe = None
        for qb4 in range(FOX_NQ):
            q0 = qb4 * 512
            nkb = 4 * (qb4 + 1)
            bo = fox_rr["o"]
            fox_rr["o"] = 1 - bo
            bod = kb.bank_tok[bo]
            sinfo = {}

            def emit_s(kbk):
                b = 2 + fox_rr["s"]
                fox_rr["s"] = (fox_rr["s"] + 1) % 6
                bd = kb.bank_tok[b]
                m = P.op("pe", lambda e, b=b, kbk=kbk, KA=KA, QA=QA, q0=q0: e.matmul(kb.psum[:, b, :], KA[0:70, kbk * 128:(kbk + 1) * 128], QA[0:70, q0:q0 + 512], start=True, stop=True),
                         [bd, tq, tk])
                sinfo[kbk] = (b, m)

            emit_s(0)
            if nkb > 1:
                emit_s(1)
            for kbk in range(nkb):
                b, m = sinfo[kbk]
                E, ef, ei = Er.get()
                d = kbk - 4 * qb4
                if d < 0:
                    te = P.op("act", lambda e, E=E, b=b: e.activation(out=E[:], in_=kb.psum[:, b, :], func=AF.Exp), [m, ef])
                    kb.bank_tok[b] = te
                else:
                    F_, ff, fi = Ef.get()
                    t1 = P.op("dve", lambda e, F_=F_, b=b, d=d: e.tensor_tensor(out=F_[:], in0=kb.psum[:, b, :], in1=dm[:, d, :], op=ALU.add), [m, ff, tdm])
                    kb.bank_tok[b] = t1
                    te = P.op("act", lambda e, E=E, F_=F_: e.activation(out=E[:], in_=F_[:], func=AF.Exp), [t1, ef])
                    Ef.done(fi, te)
                pv = P.op("pe", lambda e, E=E, kbk=kbk, nkb=nkb, bo=bo, VE=VE: e.matmul(kb.psum[:, bo, :], VE[:, kbk, :], E[:], start=(kbk == 0), stop=(kbk == nkb - 1)),
                          [te, tv] + ([bod] if kbk == 0 else []))
                Er.done(ei, pv)
                last_pe = pv
                if kbk + 2 < nkb:
                    emit_s(kbk + 2)
            Rt, rf, ri = Rr.get()
            Ot, of, oi = Or.get()
            if FOX_PH >= 9:
                n1 = P.op("dve", lambda e, Rt=Rt, bo=bo: e.reciprocal(out=Rt[:], in_=kb.psum[64:128, bo, :]), [last_pe, rf])
                n2 = P.op("dve", lambda e, Rt=Rt, Ot=Ot, bo=bo: e.tensor_tensor(out=Ot[:], in0=kb.psum[0:64, bo, :], in1=Rt[:], op=ALU.mult), [n1, of])
            else:
                n2 = P.op("act", lambda e, Ot=Ot, bo=bo: e.activation(out=Ot[:], in_=kb.psum[0:64, bo, :], func=AF.Copy), [last_pe, of, rf])
            kb.bank_tok[bo] = n2
            Rr.done(ri, n2)
            td = kb.load(o_d.ap()[h][:, q0:q0 + 512], Ot[:], deps=[n2], key=f"oo{oi}")
            Or.done(oi, td)
            kb.out_toks.append(td)
        QAr.done(qi, last_pe)
        KAr.done(ki, last_pe)
        VEr.done(vi, last_pe)
    return kb.finish()


def run_fox(q, k, v, lf, ncore=NCORE):
    jj = np.arange(128)[:, None, None]
    dd = np.arange(4)[None, :, None]
    qq = np.arange(512)[None, None, :]
    dmask = np.where(qq >= dd * 128 + jj, 0.0, -30000.0).astype(np.float32)
    maps = []
    for c in range(ncore):
        b, hq = c // 4, c % 4
        cs = slice(hq * 512, (hq + 1) * 512)
        m = dict(dmask=dmask)
        m["q"] = np.ascontiguousarray(q[b][:, cs].T.reshape(HPC, 64, SEQ))
        m["k"] = np.ascontiguousarray(k[b][:, cs].T.reshape(HPC, 64, SEQ))
        m["v"] = np.ascontiguousarray(v[b][:, cs].reshape(32, 128, HPC, 64).transpose(2, 1, 0, 3))
        m["lf"] = np.ascontiguousarray(lf[b][:, hq * 8:(hq + 1) * 8].T)
        maps.append(m)
    res = run_bass_kernel_spmd(get_prog("fox", build_fox), maps, core_ids=list(range(ncore)))
    o = np.zeros(q.shape, np.float32)
    for c in range(ncore):
        b, hq = c // 4, c % 4
        o[b][:, hq * 512:(hq + 1) * 512] = np.asarray(res.results[c]["o"]).reshape(512, SEQ).T
    return o


def build_l3b():
    kb = KB(nslots=4)
    P = kb.P
    xin = P.dram("xT", [128, DC, T], F32, kind="ExternalInput")
    oin = P.dram("oT", [128, DC, T], F32, kind="ExternalInput")
    sgin = P.dram("sgT", [128, DC, T], F32, kind="ExternalInput")
    pin = P.dram("pT", [128, 2, T], F32, kind="ExternalInput")
    w_out = P.dram("w_out", [D, D], F32, kind="ExternalInput")
    w_pp = P.dram("ple_proj", [256, D], F32, kind="ExternalInput")
    w_pg = P.dram("ple_gate", [D, D], F32, kind="ExternalInput")
    pn_d = P.dram("ple_norm", [128, DC], F32, kind="ExternalInput")
    fg_d = P.dram("final_g", [128, DC], F32, kind="ExternalInput")
    xout = P.dram("xo", [128, DC, T], F32, kind="ExternalOutput")
    X = P.sb("X", [128, DC, T], F32)
    A = P.sb("A", [128, DC, T], BF16)
    B = P.sb("B", [128, DC, T], BF16)
    pn = P.sb("pn", [128, DC], F32)
    fg = P.sb("fg", [128, DC], F32)
    pT = P.sb("pTb", [128, 2, T], BF16)
    tx = kb.load(X[:], xin.ap())
    tpn = kb.load(pn[:], pn_d.ap())
    tfg = kb.load(fg[:], fg_d.ap())
    tp = P.dma("pool", "pcast", pT[:], pin.ap())
    Orr = Rot([P.sb(f"O{i}", [128, 512], F32) for i in range(2)])
    Srr = Rot([P.sb(f"S{i}", [128, 512], F32) for i in range(2)])
    tb = None
    for c in range(DC):
        for (h0, hn) in TB2:
            O, of, oi = Orr.get()
            S, sf, si = Srr.get()
            t1 = kb.load(O[:], oin.ap()[:, c, h0:h0 + hn], deps=[of], key=f"o{oi}")
            t2 = kb.load(S[:], sgin.ap()[:, c, h0:h0 + hn], deps=[sf], key=f"s{si}")
            tb = P.op("dve", lambda e, O=O, S=S, c=c, h0=h0, hn=hn: e.tensor_tensor(out=B[:, c, h0:h0 + hn], in0=O[:], in1=S[:], op=ALU.mult), [t1, t2])
            Orr.done(oi, tb)
            Srr.done(si, tb)
    tres = kb.outproj_residual(w_out, X, B, TB2, a_deps=tb, x_deps=[tx])
    tple = kb.ple(w_pp, w_pg, 3, pn, X, A, pT, TB2, x_deps=[tres, tpn], a_free=None, p_deps=[tp])
    tfin = kb.rmsnorm(lambda c, t0, n: X[:, c, t0:t0 + n], fg, lambda c, t0, n: X[:, c, t0:t0 + n], TB2, deps=[tple, tfg])
    kb.out_toks.append(kb.load(xout.ap(), X[:], deps=[tfin]))
    return kb.finish()


def run_l3b(inp, x3, o, sg, ncore=NCORE):
    p = np.asarray(inp["p"], np.float32)
    common = dict(w_out=np.asarray(inp["fox_w_out"][0]), ple_proj=np.asarray(inp["ple_proj"][3]), ple_gate=np.asarray(inp["ple_gate"][3]),
                  ple_norm=vec_fm(np.asarray(inp["ple_norm"][3])), final_g=vec_fm(np.asarray(inp["final_g"])))
    maps = []
    for c in range(ncore):
        b, t = c // 4, c % 4
        sl = slice(t * T, (t + 1) * T)
        m = dict(common)
        m["xT"] = fm(x3[b, sl]); m["oT"] = fm(o[b, sl]); m["sgT"] = fm(sg[b, sl]); m["pT"] = fm(p[3, b, sl])
        maps.append(m)
    res = run_bass_kernel_spmd(get_prog("l3b", build_l3b), maps, core_ids=list(range(ncore)))
    out = np.zeros_like(x3)
    for c in range(ncore):
        b, t = c // 4, c % 4
        out[b, t * T:(t + 1) * T] = unfm(np.asarray(res.results[c]["xo"]))
    return out


_CACHE = {}


def get_prog(name, fn):
    if name not in _CACHE:
        _CACHE[name] = fn()
    return _CACHE[name]


REL_BUCKETS = 32
REL_MAX_DIST = 128


def t5_bucket(dist):
    max_exact = REL_BUCKETS // 2
    d = np.maximum(dist, 1).astype(np.float32)
    large = max_exact + (np.log(d / max_exact) / np.log(REL_MAX_DIST / max_exact) * (REL_BUCKETS - max_exact)).astype(np.int32)
    large = np.minimum(large, REL_BUCKETS - 1)
    return np.where(dist < max_exact, dist, large).astype(np.int32)


def run_l0(inp, stop=None, ncore=NCORE):
    x = np.asarray(inp["x"], np.float32)
    p = np.asarray(inp["p"], np.float32)
    qi = np.arange(128)[:, None]
    kj = np.arange(256)[None, :]
    dist = qi + 128 - kj
    band = ((dist >= 0) & (dist < 128)).astype(np.float32)
    bucket = t5_bucket(np.clip(dist, 0, None))
    rel = np.asarray(inp["rel_bias"], np.float32)
    hperm = np.array([kvh * 8 + 2 * i + par for kvh in range(4) for par in range(2) for i in range(4)])
    bias = rel[bucket][:, :, hperm]
    biasT = np.ascontiguousarray(bias.transpose(1, 2, 0).reshape(2, 128, 32, 128).transpose(1, 0, 2, 3))
    maskT = np.ascontiguousarray(band.T.reshape(2, 128, 128).transpose(1, 0, 2))
    sinks = np.ascontiguousarray(np.broadcast_to(np.asarray(inp["swa_sinks"], np.float32)[0][None, :], (128, 32)))
    common = dict(w_in=np.asarray(inp["swa_w_in"][0]), w_out=np.asarray(inp["swa_w_out"][0]),
                  ple_proj=np.asarray(inp["ple_proj"][0]), ple_gate=np.asarray(inp["ple_gate"][0]),
                  norm_g=vec_fm(np.asarray(inp["norm_g"][0])), ple_norm=vec_fm(np.asarray(inp["ple_norm"][0])),
                  biasT=biasT, maskT=maskT, sinks=sinks)
    maps = []
    for c in range(ncore):
        b, t = c // 4, c % 4
        xe = np.zeros((T + 128, D), np.float32)
        xe[128:] = x[b, t * T:(t + 1) * T]
        if t > 0:
            xe[:128] = x[b, t * T - 128:t * T]
        m = dict(common)
        m["xT"] = fm(xe)
        m["pT"] = fm(p[0, b, t * T:(t + 1) * T])
        m["hasprev"] = np.full((128, 1), 1.0 if t > 0 else 0.0, np.float32)
        maps.append(m)
    res = run_bass_kernel_spmd(get_prog("l0" + str(stop), lambda: build_l0(stop)), maps, core_ids=list(range(ncore)))
    x1 = np.zeros_like(x)
    for c in range(ncore):
        b, t = c // 4, c % 4
        x1[b, t * T:(t + 1) * T] = unfm(np.asarray(res.results[c]["xo"]))
    return x1


def kernel(**inputs):
    inp = inputs
    x1 = run_l0(inp)
    x2, u, sg2 = run_l1(inp, x1)
    y = run_scan(inp, u)
    x3, q, k, v, sg3, lf = run_l2b(inp, x2, y, sg2)
    o = run_fox(q, k, v, lf)
    out = run_l3b(inp, x3, o, sg3)
    return out.astype(np.float32)
```

```python
import math
import numpy as np
import ml_dtypes
import concourse.bass as bass
import concourse.mybir as mybir
from concourse.bass_utils import run_bass_kernel_spmd

F32 = mybir.dt.float32
BF16 = mybir.dt.bfloat16
I32 = mybir.dt.int32
AF = mybir.ActivationFunctionType
ALU = mybir.AluOpType
NPBF = ml_dtypes.bfloat16

D = 2048
DC = 16
T = 1024
SEQ = 4096
NCORE = 8
EPS = 1e-6
ENG_NAMES = ["pe", "act", "dve", "pool", "sp"]


class Tok:
    __slots__ = ("key", "val")

    def __init__(self, key, val):
        self.key = key
        self.val = val


class Prog:
    def __init__(self):
        self.nc = bass.Bass("TRN2", target_bir_lowering=False)
        self.q = {n: [] for n in ENG_NAMES}
        self.cnt = {}
        self.seen = {n: {} for n in ENG_NAMES}
        self.sem_keys = []
        self._ctx = []
        for n in ["pe", "act", "dve", "pool"]:
            self._newsem(n)

    def _newsem(self, key):
        if key not in self.cnt:
            self.cnt[key] = 0
            self.sem_keys.append(key)

    def sb(self, name, shape, dt):
        cm = self.nc.sbuf_tensor(name, list(shape), dt)
        t = cm.__enter__()
        self._ctx.append(cm)
        return t

    def ps(self, name, shape, dt=F32):
        cm = self.nc.psum_tensor(name, list(shape), dt)
        t = cm.__enter__()
        self._ctx.append(cm)
        return t

    def dram(self, name, shape, dt, kind=None):
        if kind is None:
            return self.nc.dram_tensor(name, list(shape), dt)
        return self.nc.dram_tensor(name, list(shape), dt, kind=kind)

    def _waits(self, engine, deps):
        ws = []
        seen = self.seen[engine]
        for d in deps:
            if d is None:
                continue
            if isinstance(d, (list, tuple)):
                ws += self._waits(engine, d)
                continue
            if seen.get(d.key, 0) >= d.val:
                continue
            seen[d.key] = d.val
            ws.append((d.key, d.val))
        return ws

    def op(self, engine, fn, deps=()):
        ws = self._waits(engine, deps)
        self.cnt[engine] += 1
        v = self.cnt[engine]
        self.q[engine].append((fn, ws, engine, 1))
        return Tok(engine, v)

    def dma(self, engine, semkey, out, in_, deps=(), **kw):
        self._newsem(semkey)
        ws = self._waits(engine, deps)
        self.cnt[semkey] += 16
        v = self.cnt[semkey]
        self.q[engine].append((lambda e: e.dma_start(out=out, in_=in_, **kw), ws, semkey, 16))
        return Tok(semkey, v)

    def wait(self, engine, deps):
        ws = self._waits(engine, deps)
        if ws:
            self.q[engine].append((None, ws, None, 0))

    def build(self):
        nc = self.nc
        sems = {}
        cms = []
        for k in self.sem_keys:
            cm = nc.semaphore("s_" + str(k))
            sems[k] = cm.__enter__()
            cms.append(cm)
        q = self.q
        with nc.Block() as block:
            def emit(name):
                def body(e):
                    for fn, ws, key, inc in q[name]:
                        for (wk, wv) in ws:
                            e.wait_ge(sems[wk], wv)
                        if fn is not None:
                            ins = fn(e)
                            ins.then_inc(sems[key], inc)
                return body
            if q["pe"]:
                block.tensor(emit("pe"))
            if q["act"]:
                block.scalar(emit("act"))
            if q["dve"]:
                block.vector(emit("dve"))
            if q["pool"]:
                block.gpsimd(emit("pool"))
            if q["sp"]:
                block.sync(emit("sp"))
        for cm in reversed(cms):
            cm.__exit__(None, None, None)
        for cm in reversed(self._ctx):
            cm.__exit__(None, None, None)
        return nc


class KB:
    def __init__(self, nslots=6, wcols=256):
        self.P = Prog()
        P = self.P
        self.psum = P.ps("psum", [128, 8, 512], F32)
        self.bank_tok = [None] * 8
        self.bank_rr = 0
        self.ones = P.sb("ones_bf", [128, 128], BF16)
        self.t_ones = P.op("dve", lambda e: e.memset(self.ones[:], 1.0))
        self.cst = P.sb("cst", [128, 8], F32)
        cvals = [EPS, math.pi / 2, 1.0, 0.0, -1.0, 0.5, 2.0, -math.pi / 2]
        tc = None
        for i, v in enumerate(cvals):
            tc = P.op("dve", lambda e, i=i, v=v: e.memset(self.cst[:, i:i + 1], v))
        self.t_cst = tc
        self.wcols = wcols
        self.wbuf = [P.sb(f"w{i}", [128, 16, wcols], BF16) for i in range(nslots)]
        self.w_free = [None] * nslots
        self.w_rr = 0
        self.sq = [P.sb(f"sq{i}", [128, 512], BF16) for i in range(2)]
        self.sq_free = [None, None]
        self.sq_rr = 0
        self.rs = [P.sb(f"rs{i}", [128, 512], F32) for i in range(2)]
        self.rs_free = [None, None]
        self.rs_rr = 0
        self.out_toks = []
        self.io_n = 0
        self.scr = P.sb("scr", [128, 2048], F32)

    def scrv(self, i):
        return self.scr[:, i * 512:(i + 1) * 512]

    def bank(self):
        b = self.bank_rr
        self.bank_rr = (b + 1) % 8
        return b, self.bank_tok[b]

    def load(self, out, in_, deps=(), eng="sp", key=None):
        if key is None:
            key = f"io{self.io_n}"
            self.io_n += 1
        return self.P.dma(eng, key, out, in_, deps)

    def wstream(self, loads):
        P = self.P
        ns = len(self.wbuf)
        pre = ns - 1
        state = {"next": 0, "info": {}}

        def issue(i):
            Wd, row0, K, col0, ncols = loads[i][:5]
            s = self.w_rr
            self.w_rr = (s + 1) % ns
            KC = K // 128
            src = Wd.ap()[row0:row0 + K, col0:col0 + ncols].rearrange("(kc p) n -> p kc n", p=128)
            t = P.dma("pool", f"w{s}", self.wbuf[s][:, 0:KC, 0:ncols], src, deps=[self.w_free[s]])
            state["info"][i] = (s, t)

        def get(i, done=0):
            while state["next"] < len(loads) and (state["next"] <= i or (state["next"] <= i + pre - 1 and state["next"] - ns < done)):
                issue(state["next"])
                state["next"] += 1
            return state["info"][i]

        return get

    def wdone(self, slot, tok):
        self.w_free[slot] = tok

    def gemm(self, specs, ngroups, tblocks, epi, gcols=None):
        P = self.P
        gcols = gcols or self.wcols
        loads = []
        for g in range(ngroups):
            for sp in specs:
                loads.append((sp["W"], sp.get("row0", 0), sp["K"], sp["col"](g), gcols))
        get = self.wstream(loads)
        nsp = len(specs)
        nch = (gcols + 127) // 128
        for g in range(ngroups):
            infos = [get(g * nsp + si, g * nsp) for si in range(nsp)]
            lasts = [None] * nsp
            for ci in range(nch):
                cw = min(128, gcols - ci * 128)
                for bi, (t0, n) in enumerate(tblocks):
                    pas, toks, banks = [], [], []
                    for si, sp in enumerate(specs):
                        s, wt = infos[si]
                        KC = sp["K"] // 128
                        b, bdep = self.bank()
                        pa = self.psum[0:cw, b, 0:n]
                        last = None
                        for kc in range(KC):
                            deps = [wt, bdep] + list(sp.get("deps", [])) if kc == 0 else []
                            last = P.op("pe", (lambda e, pa=pa, s=s, kc=kc, ci=ci, cw=cw, t0=t0, n=n, sp=sp, KC=KC:
                                               e.matmul(pa, self.wbuf[s][:, kc, ci * 128:ci * 128 + cw], sp["rhs"](kc, t0, n),
                                                        start=(kc == 0), stop=(kc == KC - 1))), deps)
                        lasts[si] = last
                        pas.append(pa)
                        toks.append(last)
                        banks.append(b)
                    rel = epi(g, ci, bi, pas, toks)
                    for b, r in zip(banks, rel):
                        self.bank_tok[b] = r
            for si in range(nsp):
                self.wdone(infos[si][0], lasts[si])

    def rmsnorm(self, Xf, g_sb, outf, tblocks, deps=(), out_deps=()):
        P = self.P
        last = None
        for (t0, n) in tblocks:
            b, bdep = self.bank()
            pa = self.psum[:, b, 0:n]
            tm = None
            for c in range(DC):
                i = self.sq_rr
                self.sq_rr = 1 - i
                sq = self.sq[i]
                ta = P.op("act", lambda e, sq=sq, c=c, t0=t0, n=n: e.activation(out=sq[:, 0:n], in_=Xf(c, t0, n), func=AF.Square),
                          [deps, self.sq_free[i]])
                tm = P.op("pe", lambda e, pa=pa, sq=sq, c=c, n=n: e.matmul(pa, self.ones[:], sq[:, 0:n], start=(c == 0), stop=(c == DC - 1)),
                          [ta, self.t_ones] + ([bdep] if c == 0 else []))
                self.sq_free[i] = tm
            j = self.rs_rr
            self.rs_rr = 1 - j
            rs = self.rs[j]
            t1 = P.op("act", lambda e, rs=rs, pa=pa, n=n: e.activation(out=rs[:, 0:n], in_=pa, func=AF.Sqrt, scale=1.0 / D, bias=self.cst[:, 0:1]),
                      [tm, self.rs_free[j], self.t_cst])
            self.bank_tok[b] = t1
            t2 = P.op("dve", lambda e, rs=rs, n=n: e.reciprocal(out=rs[:, 0:n], in_=rs[:, 0:n]), [t1])
            for c in range(DC):
                last = P.op("dve", lambda e, rs=rs, c=c, t0=t0, n=n: e.scalar_tensor_tensor(
                    out=outf(c, t0, n), in0=Xf(c, t0, n), scalar=g_sb[:, c:c + 1], in1=rs[:, 0:n], op0=ALU.mult, op1=ALU.mult),
                    [t2, deps, out_deps])
            self.rs_free[j] = last
        return last

    def outproj_residual(self, Wd, X, A, tblocks, a_deps, x_deps=()):
        P = self.P
        st = {"last": None}

        def epi(g, ci, bi, pas, toks):
            j = g * 2 + ci
            t0, n = tblocks[bi]
            t = P.op("dve", lambda e: e.tensor_tensor(out=X[:, j, t0:t0 + n], in0=X[:, j, t0:t0 + n], in1=pas[0], op=ALU.add),
                     [toks[0], x_deps])
            st["last"] = t
            return [t]

        self.gemm([dict(W=Wd, K=D, col=lambda g: g * 256, rhs=lambda kc, t0, n: A[:, kc, t0:t0 + n], deps=[a_deps])],
                  8, tblocks, epi)
        return st["last"]

    def ple(self, Wproj, Wgate, li, pn_sb, X, A, pT, tblocks, x_deps, a_free, p_deps):
        P = self.P
        tn = self.rmsnorm(lambda c, t0, n: X[:, c, t0:t0 + n], pn_sb, lambda c, t0, n: A[:, c, t0:t0 + n], tblocks,
                          deps=[x_deps], out_deps=[a_free])
        tmp = [self.scrv(0), self.scrv(1)]
        st = {"rr": 0, "free": [None, None], "last": None}

        def epi(g, ci, bi, pas, toks):
            j = g * 2 + ci
            t0, n = tblocks[bi]
            i = st["rr"]
            st["rr"] = 1 - i
            tb = tmp[i]
            t1 = P.op("act", lambda e: e.activation(out=tb[:, 0:n], in_=pas[0], func=AF.Sigmoid), [toks[0], st["free"][i]])
            t2 = P.op("dve", lambda e: e.tensor_tensor(out=tb[:, 0:n], in0=pas[1], in1=tb[:, 0:n], op=ALU.mult), [t1, toks[1]])
            t3 = P.op("dve", lambda e: e.tensor_tensor(out=X[:, j, t0:t0 + n], in0=X[:, j, t0:t0 + n], in1=tb[:, 0:n], op=ALU.add), [t2])
            st["free"][i] = t3
            st["last"] = t3
            return [t1, t2]

        self.gemm([dict(W=Wgate, K=D, row0=0, col=lambda g: g * 256, rhs=lambda kc, t0, n: A[:, kc, t0:t0 + n], deps=[tn]),
                   dict(W=Wproj, K=256, row0=0, col=lambda g: g * 256, rhs=lambda kc, t0, n: pT[:, kc, t0:t0 + n], deps=[p_deps])],
                  8, tblocks, epi)
        return st["last"]

    def finish(self):
        self.P.wait("sp", self.out_toks)
        return self.P.build()


TB2 = [(0, 512), (512, 512)]


def fm(a):
    t, f = a.shape
    return np.ascontiguousarray(a.T.reshape(f // 128, 128, t).transpose(1, 0, 2))


def unfm(a):
    p, c, t = a.shape
    return np.ascontiguousarray(a.transpose(1, 0, 2).reshape(c * p, t).T)


def vec_fm(v):
    return np.ascontiguousarray(v.reshape(-1, 128).T)


def build_l0(stop=None):
    kb = KB(nslots=3)
    P = kb.P
    HL = 128
    TE = T + HL
    xin = P.dram("xT", [128, DC, TE], F32, kind="ExternalInput")
    pin = P.dram("pT", [128, 2, T], F32, kind="ExternalInput")
    w_in = P.dram("w_in", [D, 4608], F32, kind="ExternalInput")
    w_out = P.dram("w_out", [D, D], F32, kind="ExternalInput")
    w_pp = P.dram("ple_proj", [256, D], F32, kind="ExternalInput")
    w_pg = P.dram("ple_gate", [D, D], F32, kind="ExternalInput")
    ng_d = P.dram("norm_g", [128, DC], F32, kind="ExternalInput")
    pn_d = P.dram("ple_norm", [128, DC], F32, kind="ExternalInput")
    bias_d = P.dram("biasT", [128, 2, 32, 128], F32, kind="ExternalInput")
    mask_d = P.dram("maskT", [128, 2, 128], F32, kind="ExternalInput")
    sink_d = P.dram("sinks", [128, 32], F32, kind="ExternalInput")
    hp_d = P.dram("hasprev", [128, 1], F32, kind="ExternalInput")
    xout = P.dram("xo", [128, DC, T], F32, kind="ExternalOutput")

    X = P.sb("X", [128, DC, T], F32)
    A = P.sb("A", [128, DC, TE], BF16)
    B = P.sb("B", [128, DC, T], BF16)
    Xh = kb.scr[:].rearrange("p (c t) -> p c t", c=DC)
    KT = P.sb("KT", [128, 2, TE], BF16)
    KA = P.sb("KA", [128, 2, TE], BF16)
    VE = P.sb("VE", [128, 9, 4, 128], BF16)
    M = X[:, 0:8, :].rearrange("p c t -> p (c t)").rearrange("p (k h q) -> p k h q", k=2, h=32)
    ng = P.sb("ng", [128, DC], F32)
    pn = P.sb("pn", [128, DC], F32)
    esk = P.sb("esk", [128, 32], F32)
    hp = P.sb("hp", [128, 1], F32)
    mk = P.sb("mk", [128, 2, 128], F32)
    pT = P.sb("pTb", [128, 2, T], BF16)

    tx = kb.load(X[:], xin.ap()[:, :, HL:TE])
    txh = kb.load(Xh[:], xin.ap()[:, :, 0:HL])
    tng = kb.load(ng[:], ng_d.ap())
    tpn = kb.load(pn[:], pn_d.ap())
    tmk = kb.load(mk[:], mask_d.ap())
    tsk = kb.load(esk[:], sink_d.ap())
    thp = kb.load(hp[:], hp_d.ap())
    tp = P.dma("pool", "pcast", pT[:], pin.ap())

    tsk = P.op("act", lambda e: e.activation(out=esk[:], in_=esk[:], func=AF.Exp), [tsk])
    tve = P.op("dve", lambda e: e.memset(VE[:], 1.0))

    tnh = kb.rmsnorm(lambda c, t0, n: Xh[:, c, t0:t0 + n], ng, lambda c, t0, n: A[:, c, t0:t0 + n], [(0, HL)], deps=[txh, tng])
    tno = kb.rmsnorm(lambda c, t0, n: X[:, c, t0:t0 + n], ng, lambda c, t0, n: A[:, c, HL + t0:HL + t0 + n], TB2, deps=[tx, tng])
    tn = [tnh, tno]
    def early(deps):
        kb.out_toks.append(kb.load(xout.ap(), X[:], deps=deps))
        return kb.finish()
    if stop == "norm":
        return early([tnh, tno])
    rhsA_own = lambda kc, t0, n: A[:, kc, HL + t0:HL + t0 + n]
    rhsA_ext = lambda kc, t0, n: A[:, kc, t0:t0 + n]

    st = {"q": None, "k": None}

    def epi_q(g, ci, bi, pas, toks):
        j = g * 2 + ci
        t0, n = TB2[bi]
        t = P.op("act", lambda e: e.activation(out=B[:, j, t0:t0 + n], in_=pas[0], func=AF.Copy, scale=0.125), [toks[0]])
        st["q"] = t
        return [t]

    kb.gemm([dict(W=w_in, K=D, col=lambda g: g * 256, rhs=rhsA_own, deps=tn)], 8, TB2, epi_q)

    if stop == "q":
        return early([st["q"]])
    import os
    TBE = [(0, 512), (512, 512), (1024, 128)]
    if os.environ.get("KDBG") == "2blk":
        TBE = TBE[:2]

    def epi_k(g, ci, bi, pas, toks):
        t0, n = TBE[bi]
        t = P.op("act", lambda e: e.activation(out=KT[:, ci, t0:t0 + n], in_=pas[0], func=AF.Copy), [toks[0]])
        if os.environ.get("KDBG") == "noka":
            st["k"] = [t]
            return [t]
        t2 = P.op("act", lambda e: e.activation(out=KA[0:64, ci, t0:t0 + n], in_=KT[64:128, ci, t0:t0 + n], func=AF.Copy), [t])
        t3 = P.op("act", lambda e: e.activation(out=KA[64:128, ci, t0:t0 + n], in_=KT[0:64, ci, t0:t0 + n], func=AF.Copy), [t])
        st["k"] = [t, t3]
        return [[t, t3]]

    kb.gemm([dict(W=w_in, K=D, col=lambda g: 2048, rhs=rhsA_ext, deps=tn)], 1, TBE, epi_k)

    if stop == "k":
        return early([st["k"]])
    getv = kb.wstream([(w_in, 0, D, 2304, 256)])
    sv, tv = getv(0)
    lastv = None
    tvs = []
    for blk in range(9):
        b, bdep = kb.bank()
        pa = kb.psum[:, b, 0:256]
        for kc in range(DC):
            lastv = P.op("pe", lambda e, pa=pa, kc=kc, blk=blk: e.matmul(pa, A[:, kc, blk * 128:(blk + 1) * 128], kb.wbuf[sv][:, kc, 0:256],
                                                                      start=(kc == 0), stop=(kc == DC - 1)),
                         [tv, bdep] + tn if kc == 0 else [])
        tcp = P.op("act", lambda e, pa=pa, blk=blk: e.activation(out=VE[:, blk, :, 0:64], in_=pa.rearrange("p (k d) -> p k d", k=4), func=AF.Copy),
                   [lastv, tve])
        kb.bank_tok[b] = tcp
        tvs.append(tcp)
    kb.wdone(sv, lastv)

    if stop == "kv":
        return early([st["k"], tvs])
    tb_ = kb.load(M, bias_d.ap(), deps=[tno, st["q"]])
    t = P.op("act", lambda e: e.activation(out=M, in_=M, func=AF.Exp), [tb_])
    for kbk in range(2):
        for h in range(32):
            tM = P.op("dve", lambda e, kbk=kbk, h=h: e.tensor_tensor(out=M[:, kbk, h, :], in0=M[:, kbk, h, :], in1=mk[:, kbk, :], op=ALU.mult), [t, tmk])

    ATT_N = int(os.environ.get("ATT_N", "64"))
    ATT_PH = int(os.environ.get("ATT_PH", "9"))
    if ATT_PH == 0:
        return early([tM])
    Eb = [kb.scrv(0), kb.scrv(1)]
    Pb = [P.sb(f"Pm{i}", [128, 512], BF16) for i in range(4)]
    rc = [P.sb(f"rc{i}", [64, 128], F32) for i in range(4)]
    Efree = [None, None]
    Pfree = [None] * 4
    rcfree = [None] * 4
    rr = {"e": 0, "p": 0, "r": 0}
    att_last = None
    dbg_toks = []
    itn = 0
    for qb in range(8):
        for kvh in range(4):
            for gq in range(2):
                itn += 1
                if itn > ATT_N:
                    continue
                par = gq
                h0 = kvh * 8 + par * 4
                pms = []
                for kbk in range(2):
                    b, bdep = kb.bank()
                    last = None
                    for i in range(4):
                        h = kvh * 8 + 2 * i + par
                        c = h // 2
                        r0 = par * 64
                        ksrc = KT if (kvh % 2) * 64 == r0 else KA
                        blk = qb + kbk
                        last = P.op("pe", lambda e, b=b, i=i, c=c, r0=r0, ksrc=ksrc, blk=blk, kvh=kvh, qb=qb:
                                    e.matmul(kb.psum[:, b, i * 128:(i + 1) * 128], ksrc[r0:r0 + 64, kvh // 2, blk * 128:(blk + 1) * 128],
                                             B[r0:r0 + 64, c, qb * 128:(qb + 1) * 128], start=True, stop=True),
                                    [bdep, st["q"], st["k"]] if i == 0 else [])
                    ei = rr["e"]
                    rr["e"] = 1 - ei
                    E = Eb[ei]
                    te = P.op("act", lambda e, E=E, b=b: e.activation(out=E[:], in_=kb.psum[:, b, :], func=AF.Exp), [last, Efree[ei]])
                    kb.bank_tok[b] = te
                    dbg_toks.append(te)
                    if ATT_PH == 1:
                        continue
                    pi = rr["p"]
                    rr["p"] = (pi + 1) % 4
                    Pm = Pb[pi]
                    Mv = M[:, kbk, h0:h0 + 4, :].rearrange("p h q -> p (h q)")
                    if qb == 0 and kbk == 0:
                        tpm = P.op("dve", lambda e, Pm=Pm, E=E, Mv=Mv: e.scalar_tensor_tensor(out=Pm[:], in0=E[:], scalar=hp[:, 0:1], in1=Mv,
                                                                                           op0=ALU.mult, op1=ALU.mult),
                                   [te, tM, thp, Pfree[pi]])
                    else:
                        tpm = P.op("dve", lambda e, Pm=Pm, E=E, Mv=Mv: e.tensor_tensor(out=Pm[:], in0=E[:], in1=Mv, op=ALU.mult),
                                   [te, tM, Pfree[pi]])
                    Efree[ei] = tpm
                    dbg_toks.append(tpm)
                    pms.append((pi, Pm, tpm))
                if ATT_PH <= 2:
                    continue
                b, bdep = kb.bank()
                last = None
                for kbk in range(2):
                    pi, Pm, tpm = pms[kbk]
                    blk = qb + kbk
                    last = P.op("pe", lambda e, b=b, Pm=Pm, blk=blk, kvh=kvh, kbk=kbk:
                                e.matmul(kb.psum[:, b, :], VE[:, blk, kvh, :], Pm[:], start=(kbk == 0), stop=(kbk == 1)),
                                [tpm, bdep, tvs[blk]])
                for kbk in range(2):
                    Pfree[pms[kbk][0]] = last
                dbg_toks.append(last)
                if ATT_PH <= 3:
                    continue
                for i in range(4):
                    h = kvh * 8 + 2 * i + par
                    c = h // 2
                    r0 = par * 64
                    ri = rr["r"]
                    rr["r"] = (ri + 1) % 4
                    R = rc[ri]
                    t1 = P.op("dve", lambda e, R=R, b=b, i=i, h=h: e.tensor_scalar(out=R[:], in0=kb.psum[64:128, b, i * 128:(i + 1) * 128],
                                                                                 scalar1=esk[64:128, h:h + 1], scalar2=None, op0=ALU.add),
                              [last, tsk, rcfree[ri]])
                    t2 = P.op("dve", lambda e, R=R: e.reciprocal(out=R[:], in_=R[:]), [t1])
                    if r0 == 0:
                        t3 = P.op("dve", lambda e, R=R, b=b, i=i, c=c, r0=r0, qb=qb: e.tensor_tensor(
                            out=B[0:64, c, qb * 128:(qb + 1) * 128], in0=kb.psum[0:64, b, i * 128:(i + 1) * 128], in1=R[:], op=ALU.mult), [t2])
                        t4 = t3
                    else:
                        t3 = P.op("dve", lambda e, R=R, b=b, i=i: e.tensor_tensor(
                            out=R[:], in0=kb.psum[0:64, b, i * 128:(i + 1) * 128], in1=R[:], op=ALU.mult), [t2])
                        t4 = P.op("act", lambda e, R=R, c=c, qb=qb: e.activation(out=B[64:128, c, qb * 128:(qb + 1) * 128], in_=R[:], func=AF.Copy), [t3])
                    rcfree[ri] = t4
                    att_last = [t3, t4]
                kb.bank_tok[b] = att_last

    if stop == "att":
        tx2 = kb.load(X[:], xin.ap()[:, :, HL:TE], deps=[att_last, last, dbg_toks])
        return early([tx2])
    sg = [kb.scrv(2), kb.scrv(3)]
    last_pv = last
    stg = {"rr": 0, "free": [None, None], "last": None}

    def epi_g(g, ci, bi, pas, toks):
        j = g * 2 + ci
        t0, n = TB2[bi]
        i = stg["rr"]
        stg["rr"] = 1 - i
        S = sg[i]
        t1 = P.op("act", lambda e: e.activation(out=S[:, 0:n], in_=pas[0], func=AF.Silu), [toks[0], stg["free"][i]])
        t2 = P.op("dve", lambda e: e.tensor_tensor(out=B[:, j, t0:t0 + n], in0=B[:, j, t0:t0 + n], in1=S[:, 0:n], op=ALU.mult), [t1, att_last])
        stg["free"][i] = t2
        stg["last"] = t2
        return [t1]

    kb.gemm([dict(W=w_in, K=D, col=lambda g: 2560 + g * 256, rhs=rhsA_own, deps=tn)], 8, TB2, epi_g)

    if stop == "gate":
        tx2 = kb.load(X[:], xin.ap()[:, :, HL:TE], deps=[att_last, last_pv, stg["last"]])
        return early([tx2])
    tx2 = kb.load(X[:], xin.ap()[:, :, HL:TE], deps=[att_last, last_pv])
    tres = kb.outproj_residual(w_out, X, B, TB2, a_deps=stg["last"], x_deps=[tx2])
    if stop == "out":
        return early([tres])
    tple = kb.ple(w_pp, w_pg, 0, pn, X, A, pT, TB2, x_deps=[tres, tpn], a_free=[tres], p_deps=[tp])
    kb.out_toks.append(kb.load(xout.ap(), X[:], deps=[tple]))
    return kb.finish()


def build_l1():
    kb = KB(nslots=5)
    P = kb.P
    HL = 2
    TE = T + HL
    xin = P.dram("xT", [128, DC, TE], F32, kind="ExternalInput")
    pin = P.dram("pT", [128, 2, T], F32, kind="ExternalInput")
    w_in = P.dram("w_in", [D, 8192], F32, kind="ExternalInput")
    w_out = P.dram("w_out", [D, D], F32, kind="ExternalInput")
    w_pp = P.dram("ple_proj", [256, D], F32, kind="ExternalInput")
    w_pg = P.dram("ple_gate", [D, D], F32, kind="ExternalInput")
    ng_d = P.dram("norm_g", [128, DC], F32, kind="ExternalInput")
    pn_d = P.dram("ple_norm", [128, DC], F32, kind="ExternalInput")
    ck_d = P.dram("convk", [128, DC, 3], F32, kind="ExternalInput")
    w2_in = P.dram("w2_in", [D, 4096], F32, kind="ExternalInput")
    ng2_d = P.dram("norm_g2", [128, DC], F32, kind="ExternalInput")
    xout = P.dram("xo", [128, DC, T], F32, kind="ExternalOutput")
    uout = P.dram("uo", [128, DC, T], F32, kind="ExternalOutput")
    sout = P.dram("so", [128, DC, T], F32, kind="ExternalOutput")

    X = P.sb("X", [128, DC, T], F32)
    A = P.sb("A", [128, DC, TE], BF16)
    Braw = P.sb("Braw", [128, 8192], F32)
    B = Braw[:].bitcast(BF16).rearrange("p (c t) -> p c t", c=DC)
    ST = [Braw[:, i * 1024:(i + 1) * 1024] for i in range(8)]
    Xh = P.sb("Xh", [128, DC, HL], F32)
    Z = [P.sb(f"Z{i}", [128, 2, TE], F32) for i in range(2)]
    ng = P.sb("ng", [128, DC], F32)
    pn = P.sb("pn", [128, DC], F32)
    ng2 = P.sb("ng2", [128, DC], F32)
    ck = P.sb("ck", [128, DC, 3], F32)
    pT = P.sb("pTb", [128, 2, T], BF16)

    tx = kb.load(X[:], xin.ap()[:, :, HL:TE])
    txh = kb.load(Xh[:], xin.ap()[:, :, 0:HL])
    tng = kb.load(ng[:], ng_d.ap())
    tpn = kb.load(pn[:], pn_d.ap())
    tng2 = kb.load(ng2[:], ng2_d.ap())
    tck = kb.load(ck[:], ck_d.ap())
    tp = P.dma("pool", "pcast", pT[:], pin.ap())

    tnh = kb.rmsnorm(lambda c, t0, n: Xh[:, c, t0:t0 + n], ng, lambda c, t0, n: A[:, c, t0:t0 + n], [(0, HL)], deps=[txh, tng])
    tno = kb.rmsnorm(lambda c, t0, n: X[:, c, t0:t0 + n], ng, lambda c, t0, n: A[:, c, HL + t0:HL + t0 + n], TB2, deps=[tx, tng])
    tn = [tnh, tno]
    rhsA_ext = lambda kc, t0, n: A[:, kc, t0:t0 + n]
    TBE = [(0, HL), (HL, 512), (HL + 512, 512)]
    tA, tB_, tC = kb.scrv(0), kb.scrv(1), kb.scrv(2)
    stc = {"fa": None, "fb": None, "fc": None, "z": {}, "last": None, "zfree": [None, None]}

    def epi_conv(g, ci, bi, pas, toks):
        j = g * 2 + ci
        t0, n = TBE[bi]
        Zt = Z[g % 2]
        t1 = P.op("act", lambda e: e.activation(out=tA[:, 0:n], in_=pas[0], func=AF.Copy), [toks[0], stc["fa"]])
        zdeps = [t1, toks[1]]
        if bi == 0 and ci == 0:
            zdeps.append(stc["zfree"][g % 2])
        t2 = P.op("dve", lambda e: e.tensor_tensor(out=Zt[:, ci, t0:t0 + n], in0=pas[1], in1=tA[:, 0:n], op=ALU.mult), zdeps)
        stc["fa"] = t2
        stc["z"][(g, ci, bi)] = t2
        if bi == 0:
            return [t1, t2, toks[2], toks[3]]
        zp = stc["z"][(g, ci, bi - 1)]
        t3 = P.op("dve", lambda e: e.tensor_scalar(out=tB_[:, 0:n], in0=Zt[:, ci, t0:t0 + n], scalar1=ck[:, j, 2:3], scalar2=None, op0=ALU.mult),
                  [t2, zp, tck, stc["fb"]])
        t4 = P.op("dve", lambda e: e.scalar_tensor_tensor(out=tB_[:, 0:n], in0=Zt[:, ci, t0 - 1:t0 - 1 + n], scalar=ck[:, j, 1:2], in1=tB_[:, 0:n],
                                                         op0=ALU.mult, op1=ALU.add), [t3])
        t5 = P.op("dve", lambda e: e.scalar_tensor_tensor(out=tB_[:, 0:n], in0=Zt[:, ci, t0 - 2:t0 - 2 + n], scalar=ck[:, j, 0:1], in1=tB_[:, 0:n],
                                                         op0=ALU.mult, op1=ALU.add), [t4])
        t6 = P.op("dve", lambda e: e.tensor_tensor(out=tB_[:, 0:n], in0=pas[2], in1=tB_[:, 0:n], op=ALU.mult), [t5, toks[2]])
        t7 = P.op("act", lambda e: e.activation(out=tC[:, 0:n], in_=pas[3], func=AF.Silu), [toks[3], stc["fc"]])
        t8 = P.op("dve", lambda e: e.tensor_tensor(out=B[:, j, t0 - HL:t0 - HL + n], in0=tB_[:, 0:n], in1=tC[:, 0:n], op=ALU.mult), [t6, t7])
        stc["fb"] = t8
        stc["fc"] = t8
        stc["last"] = t8
        if bi == 2 and ci == 1:
            stc["zfree"][g % 2] = t8
        return [t1, t2, t6, t7]

    kb.gemm([dict(W=w_in, K=D, col=lambda g: 2048 + g * 256, rhs=rhsA_ext, deps=tn),
             dict(W=w_in, K=D, col=lambda g: 4096 + g * 256, rhs=rhsA_ext, deps=tn),
             dict(W=w_in, K=D, col=lambda g: g * 256, rhs=rhsA_ext, deps=tn),
             dict(W=w_in, K=D, col=lambda g: 6144 + g * 256, rhs=rhsA_ext, deps=tn)], 8, TBE, epi_conv)

    tres = kb.outproj_residual(w_out, X, B, TB2, a_deps=stc["last"], x_deps=[tx])
    tple = kb.ple(w_pp, w_pg, 1, pn, X, A, pT, TB2, x_deps=[tres, tpn], a_free=[tres], p_deps=[tp])
    kb.out_toks.append(kb.load(xout.ap(), X[:], deps=[tple]))

    tn2 = kb.rmsnorm(lambda c, t0, n: X[:, c, t0:t0 + n], ng2, lambda c, t0, n: A[:, c, t0:t0 + n], TB2, deps=[tple, tng2], out_deps=[tple])
    sts = {"rr": 0, "free": [None] * 8}

    def epi_front(g, ci, bi, pas, toks):
        j = g * 2 + ci
        t0, n = TB2[bi]
        rel = []
        for si, (func, dst) in enumerate([(AF.Copy, uout), (AF.Silu, sout)]):
            if bi == 0:
                i = sts["rr"]
                sts["rr"] = (i + 1) % 8
                sts[("slot", si)] = i
            i = sts[("slot", si)]
            S = ST[i]
            t1 = P.op("act", lambda e, S=S, si=si, func=func: e.activation(out=S[:, t0:t0 + n], in_=pas[si], func=func),
                      [toks[si], sts["free"][i], tres])
            rel.append(t1)
            if bi == 1:
                td = kb.load(dst.ap()[:, j, :], S, deps=[t1], key=f"st{i}")
                sts["free"][i] = td
                kb.out_toks.append(td)
        return rel

    rhsA = lambda kc, t0, n: A[:, kc, t0:t0 + n]
    kb.gemm([dict(W=w2_in, K=D, col=lambda g: g * 256, rhs=rhsA, deps=[tn2]),
             dict(W=w2_in, K=D, col=lambda g: 2048 + g * 256, rhs=rhsA, deps=[tn2])], 8, TB2, epi_front)
    return kb.finish()


def tok_maps(arrs_fn, ncore):
    return [arrs_fn(c // 4, c % 4) for c in range(ncore)]


def run_l1(inp, x1, ncore=NCORE):
    p = np.asarray(inp["p"], np.float32)
    common = dict(w_in=np.asarray(inp["conv_w_in"][0]), w_out=np.asarray(inp["conv_w_out"][0]),
                  ple_proj=np.asarray(inp["ple_proj"][1]), ple_gate=np.asarray(inp["ple_gate"][1]),
                  norm_g=vec_fm(np.asarray(inp["norm_g"][1])), ple_norm=vec_fm(np.asarray(inp["ple_norm"][1])),
                  convk=np.ascontiguousarray(np.asarray(inp["conv_kernel"][0]).T.reshape(DC, 128, 3).transpose(1, 0, 2)),
                  w2_in=np.asarray(inp["ssm_w_in"][0]), norm_g2=vec_fm(np.asarray(inp["norm_g"][2])))
    maps = []
    for c in range(ncore):
        b, t = c // 4, c % 4
        xe = np.zeros((T + 2, D), np.float32)
        xe[2:] = x1[b, t * T:(t + 1) * T]
        if t > 0:
            xe[:2] = x1[b, t * T - 2:t * T]
        m = dict(common)
        m["xT"] = fm(xe)
        m["pT"] = fm(p[1, b, t * T:(t + 1) * T])
        maps.append(m)
    res = run_bass_kernel_spmd(get_prog("l1", build_l1), maps, core_ids=list(range(ncore)))
    x2 = np.zeros_like(x1)
    u = np.zeros_like(x1)
    sg = np.zeros_like(x1)
    for c in range(ncore):
        b, t = c // 4, c % 4
        x2[b, t * T:(t + 1) * T] = unfm(np.asarray(res.results[c]["xo"]))
        u[b, t * T:(t + 1) * T] = unfm(np.asarray(res.results[c]["uo"]))
        sg[b, t * T:(t + 1) * T] = unfm(np.asarray(res.results[c]["so"]))
    return x2, u, sg


class Rot:
    def __init__(self, tiles):
        self.tiles = tiles
        self.free = [None] * len(tiles)
        self.i = 0

    def get(self):
        i = self.i
        self.i = (i + 1) % len(self.tiles)
        return self.tiles[i], self.free[i], i

    def done(self, i, tok):
        self.free[i] = tok


NPAIR = 16
SB = 512


def build_scan():
    kb = KB(nslots=1, wcols=128)
    P = kb.P
    u_d = P.dram("u", [NPAIR, 32, SEQ], F32, kind="ExternalInput")
    lrc_d = P.dram("lr_c", [128, NPAIR], F32, kind="ExternalInput")
    lic_d = P.dram("li_c", [128, NPAIR], F32, kind="ExternalInput")
    ldc_d = P.dram("ld_c", [128, NPAIR], F32, kind="ExternalInput")
    lrr_d = P.dram("lr_r", [32, NPAIR * 128], F32, kind="ExternalInput")
    lir_d = P.dram("li_r", [32, NPAIR * 128], F32, kind="ExternalInput")
    ldr_d = P.dram("ld_r", [32, NPAIR * 128], F32, kind="ExternalInput")
    bre_d = P.dram("b_re", [32, NPAIR * 128], F32, kind="ExternalInput")
    bim_d = P.dram("b_im", [32, NPAIR * 128], F32, kind="ExternalInput")
    cre_d = P.dram("c_re", [128, NPAIR * 32], F32, kind="ExternalInput")
    cim_d = P.dram("c_im", [128, NPAIR * 32], F32, kind="ExternalInput")
    dsk_d = P.dram("dskin", [32, NPAIR], F32, kind="ExternalInput")
    iot_d = P.dram("iotain", [128, SB + 1], F32, kind="ExternalInput")
    y_d = P.dram("y", [NPAIR, 32, SEQ], F32, kind="ExternalOutput")

    TWO_PI = 2.0 * math.pi

    def sincos_into(x, n, rows, scratch, sn, cs, deps):
        ki, kf, s2, c2 = scratch
        t = P.op("dve", lambda e: e.tensor_scalar(out=ki[0:rows, 0:n], in0=x, scalar1=1.0 / TWO_PI, scalar2=None, op0=ALU.mult), deps)
        t = P.op("dve", lambda e: e.tensor_copy(out=kf[0:rows, 0:n], in_=ki[0:rows, 0:n]), [t])
        t = P.op("dve", lambda e: e.scalar_tensor_tensor(out=kf[0:rows, 0:n], in0=kf[0:rows, 0:n], scalar=-TWO_PI, in1=x, op0=ALU.mult, op1=ALU.add), [t])
        ta = P.op("act", lambda e: e.activation(out=s2[0:rows, 0:n], in_=kf[0:rows, 0:n], func=AF.Sin, scale=0.5), [t])
        tb = P.op("act", lambda e: e.activation(out=c2[0:rows, 0:n], in_=kf[0:rows, 0:n], func=AF.Sin, scale=0.25), [t])
        tb = P.op("dve", lambda e: e.tensor_tensor(out=c2[0:rows, 0:n], in0=c2[0:rows, 0:n], in1=c2[0:rows, 0:n], op=ALU.mult), [tb])
        tb = P.op("dve", lambda e: e.tensor_scalar(out=c2[0:rows, 0:n], in0=c2[0:rows, 0:n], scalar1=-2.0, scalar2=1.0, op0=ALU.mult, op1=ALU.add), [tb])
        t1 = P.op("dve", lambda e: e.scalar_tensor_tensor(out=sn[0:rows, 0:n], in0=s2[0:rows, 0:n], scalar=2.0, in1=c2[0:rows, 0:n], op0=ALU.mult, op1=ALU.mult), [ta, tb])
        t2 = P.op("dve", lambda e: e.tensor_tensor(out=cs[0:rows, 0:n], in0=s2[0:rows, 0:n], in1=s2[0:rows, 0:n], op=ALU.mult), [t1])
        t3 = P.op("dve", lambda e: e.tensor_scalar(out=cs[0:rows, 0:n], in0=cs[0:rows, 0:n], scalar1=-2.0, scalar2=1.0, op0=ALU.mult, op1=ALU.add), [t2])
        return sn, cs, t3

    NT = SB + 1
    scratch = (P.sb("tb_ki", [128, NT], I32), P.sb("tb_kf", [128, NT], F32), P.sb("tb_s2", [128, NT], F32), P.sb("tb_c2", [128, NT], F32))
    pm = {k: P.sb("pm_" + k, [128, NT], F32) for k in ["lr", "li", "dt", "sn", "cs", "abr", "abi", "den", "tmp"]}

    def param_math(rows, n, lr_ap, li_ap, ld_ap, out_r, out_th, out_cre, out_cim, deps):
        v = lambda k: pm[k][0:rows, 0:n]
        lr, li, dt = v("lr"), v("li"), v("dt")
        t1 = kb.load(lr, lr_ap, deps=deps)
        t2 = kb.load(li, li_ap, deps=deps)
        t3 = kb.load(dt, ld_ap, deps=deps)
        t = P.op("act", lambda e: e.activation(out=dt, in_=dt, func=AF.Exp), [t3])
        t = P.op("dve", lambda e: e.tensor_tensor(out=out_r, in0=lr, in1=dt, op=ALU.mult), [t, t1, deps])
        tth = P.op("dve", lambda e: e.tensor_tensor(out=out_th, in0=li, in1=dt, op=ALU.mult), [t, t2])
        tr = P.op("act", lambda e: e.activation(out=out_r, in_=out_r, func=AF.Exp), [t])
        sn, cs, ts = sincos_into(out_th, n, rows, scratch, pm["sn"], pm["cs"], [tth])
        sn, cs = sn[0:rows, 0:n], cs[0:rows, 0:n]
        abr, abi, den, tmp = v("abr"), v("abi"), v("den"), v("tmp")
        cr_, ci_ = out_cre, out_cim
        t = P.op("dve", lambda e: e.tensor_tensor(out=abr, in0=out_r, in1=cs, op=ALU.mult), [tr, ts])
        t = P.op("dve", lambda e: e.tensor_tensor(out=abi, in0=out_r, in1=sn, op=ALU.mult), [t])
        t = P.op("dve", lambda e: e.tensor_scalar(out=abr, in0=abr, scalar1=-1.0, scalar2=None, op0=ALU.add), [t])
        t = P.op("dve", lambda e: e.tensor_tensor(out=den, in0=lr, in1=lr, op=ALU.mult), [t])
        t = P.op("dve", lambda e: e.tensor_tensor(out=tmp, in0=li, in1=li, op=ALU.mult), [t])
        t = P.op("dve", lambda e: e.tensor_tensor(out=den, in0=den, in1=tmp, op=ALU.add), [t])
        t = P.op("dve", lambda e: e.reciprocal(out=den, in_=den), [t])
        t = P.op("dve", lambda e: e.tensor_tensor(out=cr_, in0=abr, in1=lr, op=ALU.mult), [t])
        t = P.op("dve", lambda e: e.tensor_tensor(out=tmp, in0=abi, in1=li, op=ALU.mult), [t])
        t = P.op("dve", lambda e: e.tensor_tensor(out=cr_, in0=cr_, in1=tmp, op=ALU.add), [t])
        t = P.op("dve", lambda e: e.tensor_tensor(out=cr_, in0=cr_, in1=den, op=ALU.mult), [t])
        t = P.op("dve", lambda e: e.tensor_tensor(out=ci_, in0=abi, in1=lr, op=ALU.mult), [t])
        t = P.op("dve", lambda e: e.tensor_tensor(out=tmp, in0=abr, in1=li, op=ALU.mult), [t])
        t = P.op("dve", lambda e: e.tensor_tensor(out=ci_, in0=ci_, in1=tmp, op=ALU.subtract), [t])
        t = P.op("dve", lambda e: e.tensor_tensor(out=ci_, in0=ci_, in1=den, op=ALU.mult), [t])
        return t

    pc = {k: P.sb("pc_" + k, [128, NPAIR], F32) for k in ["r", "th", "cre", "cim"]}
    tpc = param_math(128, NPAIR, lrc_d.ap(), lic_d.ap(), ldc_d.ap(), pc["r"][:], pc["th"][:], pc["cre"][:], pc["cim"][:], [])
    pr = {k: P.sb("pr_" + k, [32, NPAIR * 128], F32) for k in ["cre", "cim"]}
    prj = {k: P.sb("prj_" + k, [32, 512], F32) for k in ["r", "th"]}
    tpr = tpc
    for q in range(4):
        cs_ = slice(q * 512, (q + 1) * 512)
        tpr = param_math(32, 512, lrr_d.ap()[:, cs_], lir_d.ap()[:, cs_], ldr_d.ap()[:, cs_], prj["r"][:], prj["th"][:],
                         pr["cre"][:, cs_], pr["cim"][:, cs_], [tpr])

    Bre = P.sb("Bre", [32, NPAIR * 128], F32)
    Bim = P.sb("Bim", [32, NPAIR * 128], F32)
    bbr = P.sb("bbr", [32, NPAIR * 128], BF16)
    bbi = P.sb("bbi", [32, NPAIR * 128], BF16)
    tb1 = kb.load(Bre[:], bre_d.ap())
    tb2 = kb.load(Bim[:], bim_d.ap())
    tmp = P.sb("bb_tmp", [32, NPAIR * 128], F32)
    tmp2 = P.sb("bb_tmp2", [32, NPAIR * 128], F32)
    t = P.op("dve", lambda e: e.tensor_tensor(out=tmp[:], in0=Bre[:], in1=pr["cre"][:], op=ALU.mult), [tpr, tb1])
    t = P.op("dve", lambda e: e.tensor_tensor(out=tmp2[:], in0=Bim[:], in1=pr["cim"][:], op=ALU.mult), [t, tb2])
    t = P.op("dve", lambda e: e.tensor_tensor(out=bbr[:], in0=tmp[:], in1=tmp2[:], op=ALU.subtract), [t])
    t = P.op("dve", lambda e: e.tensor_tensor(out=tmp[:], in0=Bim[:], in1=pr["cre"][:], op=ALU.mult), [t])
    t = P.op("dve", lambda e: e.tensor_tensor(out=tmp2[:], in0=Bre[:], in1=pr["cim"][:], op=ALU.mult), [t])
    tbb = P.op("dve", lambda e: e.tensor_tensor(out=bbi[:], in0=tmp[:], in1=tmp2[:], op=ALU.add), [t])

    Cf = P.sb("Cf", [128, NPAIR * 32], F32)
    Cf2 = P.sb("Cf2", [128, NPAIR * 32], F32)
    Cre = P.sb("Cre", [128, NPAIR * 32], BF16)
    nCim = P.sb("nCim", [128, NPAIR * 32], BF16)
    tc1 = kb.load(Cf[:], cre_d.ap())
    tc2 = kb.load(Cf2[:], cim_d.ap())
    tcc = P.op("act", lambda e: e.activation(out=Cre[:], in_=Cf[:], func=AF.Copy), [tc1])
    tcc2 = P.op("act", lambda e: e.activation(out=nCim[:], in_=Cf2[:], func=AF.Copy, scale=-1.0), [tc2])
    nCre = P.sb("nCre", [128, NPAIR * 32], BF16)
    tcc3 = P.op("act", lambda e: e.activation(out=nCre[:], in_=Cf[:], func=AF.Copy, scale=-1.0), [tc1])
    dsk = P.sb("dsk", [32, NPAIR], F32)
    tdk = kb.load(dsk[:], dsk_d.ap())
    iot = P.sb("iot", [128, SB + 1], F32)
    tio = kb.load(iot[:], iot_d.ap())

    ang = P.sb("ang", [128, NT], F32)
    Ctr = Rot([P.sb(f"Ct{i}", [128, NT], F32) for i in range(2)])
    Str = Rot([P.sb(f"St{i}", [128, NT], F32) for i in range(2)])
    Ur = Rot([P.sb(f"U{i}", [32, SEQ], F32) for i in range(1)])
    Ubr = Rot([P.sb(f"Ub{i}", [32, SEQ], BF16) for i in range(2)])
    BRr = Rot([P.sb(f"BR{i}", [128, SB], F32) for i in range(2)])
    BIr = Rot([P.sb(f"BI{i}", [128, SB], F32) for i in range(2)])
    MRr = Rot([P.sb(f"MR{i}", [128, SB], F32) for i in range(2)])
    MIr = Rot([P.sb(f"MI{i}", [128, SB], F32) for i in range(2)])
    T1r = Rot([P.sb(f"T1{i}", [128, SB], F32) for i in range(2)])
    T2r = Rot([P.sb(f"T2{i}", [128, SB], F32) for i in range(2)])
    WRr = Rot([P.sb(f"WR{i}", [128, SB], F32) for i in range(2)])
    WIr = Rot([P.sb(f"WI{i}", [128, SB], F32) for i in range(2)])
    PRr = Rot([P.sb(f"PR{i}", [128, SB], BF16) for i in range(8)])
    nS = P.sb("nS", [128, 2], F32)
    ctmp_free = [None]
    car_rd = [None, None]
    YOr = Rot([P.sb(f"YO{i}", [32, SB], F32) for i in range(2)])
    car = [P.sb(f"car{i}", [128, 2], F32) for i in range(2)]
    ctmp = P.sb("ctmp", [128, 2], F32)
    tz = P.op("dve", lambda e: e.memset(car[0][:], 0.0))
    car_tok = [tz, None]
    ang_free = None
    last_ct = None
    NBLK = SEQ // SB
    for p in range(NPAIR):
        U, uf, ui = Ur.get()
        tu = kb.load(U[:], u_d.ap()[p], deps=[uf], key=f"u{ui}")
        Ub, ubf, ubi = Ubr.get()
        tub = P.dma("pool", f"ub{ubi}", Ub[:], u_d.ap()[p], deps=[ubf])
        Ct, cf, cti = Ctr.get()
        St, sf, sti = Str.get()
        ta_ = P.op("dve", lambda e, p=p: e.tensor_scalar(out=ang[:], in0=iot[:], scalar1=pc["th"][:, p:p + 1], scalar2=None, op0=ALU.mult),
                   [tio, tpc, ang_free])
        _, _, ttab = sincos_into(ang[:], NT, 128, scratch, St, Ct, [ta_, cf, sf, tpr])
        ang_free = ttab
        tns = P.op("dve", lambda e, St=St, cti=cti: e.tensor_scalar(out=nS[:, cti:cti + 1], in0=St[:, SB:SB + 1], scalar1=-1.0, scalar2=None, op0=ALU.mult), [ttab, last_ct])
        rcol = pc["r"][:, p:p + 1]
        last_pe_u = None
        last_dve_u = None
        for k in range(NBLK):
            c0 = k * SB
            b1, bd1 = kb.bank()
            b2, bd2 = kb.bank()
            m1 = P.op("pe", lambda e, b1=b1, Ub=Ub, p=p, c0=c0: e.matmul(kb.psum[:, b1, :], bbr[:, p * 128:(p + 1) * 128], Ub[:, c0:c0 + SB], start=True, stop=True),
                      [bd1, tbb, tub])
            m2 = P.op("pe", lambda e, b2=b2, Ub=Ub, p=p, c0=c0: e.matmul(kb.psum[:, b2, :], bbi[:, p * 128:(p + 1) * 128], Ub[:, c0:c0 + SB], start=True, stop=True),
                      [bd2])
            last_pe_u = m2
            C_, S_ = Ct[:, 0:SB], St[:, 0:SB]
            MR, mrf, mri = MRr.get()
            MI, mif, mii = MIr.get()
            T1, t1f, t1i = T1r.get()
            T2, t2f, t2i = T2r.get()
            g1 = P.op("dve", lambda e, MR=MR, b1=b1, C_=C_: e.tensor_tensor(out=MR[:], in0=kb.psum[:, b1, :], in1=C_, op=ALU.mult), [m1, ttab, mrf])
            g2 = P.op("dve", lambda e, T1=T1, b2=b2, S_=S_: e.tensor_tensor(out=T1[:], in0=kb.psum[:, b2, :], in1=S_, op=ALU.mult), [m2, t1f])
            g3 = P.op("dve", lambda e, MR=MR, T1=T1: e.tensor_tensor(out=MR[:], in0=MR[:], in1=T1[:], op=ALU.add), [g2])
            d1 = P.op("dve", lambda e, MI=MI, b2=b2, C_=C_: e.tensor_tensor(out=MI[:], in0=kb.psum[:, b2, :], in1=C_, op=ALU.mult), [g3, mif])
            d2 = P.op("dve", lambda e, T2=T2, b1=b1, S_=S_: e.tensor_tensor(out=T2[:], in0=kb.psum[:, b1, :], in1=S_, op=ALU.mult), [d1, t2f])
            d3 = P.op("dve", lambda e, MI=MI, T2=T2: e.tensor_tensor(out=MI[:], in0=MI[:], in1=T2[:], op=ALU.subtract), [d2])
            kb.bank_tok[b1] = d2
            kb.bank_tok[b2] = d1
            T1r.done(t1i, g3)
            T2r.done(t2i, d3)
            WR, wrf, wri = WRr.get()
            WI, wif, wii = WIr.get()
            cin = car[k % 2] if k > 0 else car[0]
            cdep = car_tok[k % 2] if k > 0 else tz
            if k == 0:
                s1 = P.op("dve", lambda e, WR=WR, MR=MR, rcol=rcol: e.tensor_tensor_scan(out=WR[:], data0=rcol.to_broadcast([128, SB]), data1=MR[:], initial=0.0, op0=ALU.mult, op1=ALU.add),
                          [g3, wrf, tpc])
                s2_ = P.op("dve", lambda e, WI=WI, MI=MI, rcol=rcol: e.tensor_tensor_scan(out=WI[:], data0=rcol.to_broadcast([128, SB]), data1=MI[:], initial=0.0, op0=ALU.mult, op1=ALU.add),
                           [d3, wif])
            else:
                s1 = P.op("dve", lambda e, WR=WR, MR=MR, rcol=rcol, cin=cin: e.tensor_tensor_scan(out=WR[:], data0=rcol.to_broadcast([128, SB]), data1=MR[:], initial=cin[:, 0:1], op0=ALU.mult, op1=ALU.add),
                          [g3, wrf, cdep])
                s2_ = P.op("dve", lambda e, WI=WI, MI=MI, rcol=rcol, cin=cin: e.tensor_tensor_scan(out=WI[:], data0=rcol.to_broadcast([128, SB]), data1=MI[:], initial=cin[:, 1:2], op0=ALU.mult, op1=ALU.add),
                           [d3, wif])
            MRr.done(mri, s1)
            MIr.done(mii, s2_)
            if k < NBLK - 1:
                cn = car[(k + 1) % 2]
                C5, S5 = Ct[:, SB:SB + 1], St[:, SB:SB + 1]
                q1 = P.op("dve", lambda e, WI=WI, S5=S5: e.tensor_scalar(out=ctmp[:, 0:1], in0=WI[:, SB - 1:SB], scalar1=S5, scalar2=None, op0=ALU.mult), [s2_, s1])
                q2 = P.op("dve", lambda e, WR=WR, C5=C5, cn=cn: e.scalar_tensor_tensor(out=cn[:, 0:1], in0=WR[:, SB - 1:SB], scalar=C5, in1=ctmp[:, 0:1], op0=ALU.mult, op1=ALU.subtract), [q1])
                q3 = P.op("dve", lambda e, WI=WI, C5=C5: e.tensor_scalar(out=ctmp[:, 1:2], in0=WI[:, SB - 1:SB], scalar1=C5, scalar2=None, op0=ALU.mult), [q2])
                q4 = P.op("dve", lambda e, WR=WR, S5=S5, cn=cn: e.scalar_tensor_tensor(out=cn[:, 1:2], in0=WR[:, SB - 1:SB], scalar=S5, in1=ctmp[:, 1:2], op0=ALU.mult, op1=ALU.add), [q3])
                car_tok[(k + 1) % 2] = q4
                ctmp_free[0] = q4
            car_rd[k % 2] = s2_
            PAr_ = [PRr.get() for _ in range(4)]
            (PA, paf, pai), (PB, pbf, pbi), (PC, pcf, pci), (PD, pdf, pdi) = PAr_
            h1 = P.op("pool", lambda e, PA=PA, WR=WR, C_=C_: e.tensor_tensor(out=PA[:], in0=WR[:], in1=C_, op=ALU.mult), [s1, paf])
            h2 = P.op("pool", lambda e, PB=PB, WI=WI, S_=S_: e.tensor_tensor(out=PB[:], in0=WI[:], in1=S_, op=ALU.mult), [s2_, pbf])
            f1 = P.op("dve", lambda e, PC=PC, WR=WR, S_=S_: e.tensor_tensor(out=PC[:], in0=WR[:], in1=S_, op=ALU.mult), [s1, pcf])
            f2 = P.op("dve", lambda e, PD=PD, WI=WI, C_=C_: e.tensor_tensor(out=PD[:], in0=WI[:], in1=C_, op=ALU.mult), [s2_, pdf])
            last_ct = [h1, h2, f1, f2]
            wdeps = [h1, h2, f1, f2]
            if k < NBLK - 1:
                wdeps.append(q4)
            WRr.done(wri, wdeps)
            WIr.done(wii, wdeps)
            b3, bd3 = kb.bank()
            m3 = P.op("pe", lambda e, b3=b3, PA=PA, p=p: e.matmul(kb.psum[0:32, b3, :], Cre[:, p * 32:(p + 1) * 32], PA[:], start=True, stop=False), [bd3, h1, tcc])
            m3b = P.op("pe", lambda e, b3=b3, PB=PB, p=p: e.matmul(kb.psum[0:32, b3, :], nCre[:, p * 32:(p + 1) * 32], PB[:], start=False, stop=False), [h2, tcc3])
            m3c = P.op("pe", lambda e, b3=b3, PC=PC, p=p: e.matmul(kb.psum[0:32, b3, :], nCim[:, p * 32:(p + 1) * 32], PC[:], start=False, stop=False), [f1, tcc2])
            m4 = P.op("pe", lambda e, b3=b3, PD=PD, p=p: e.matmul(kb.psum[0:32, b3, :], nCim[:, p * 32:(p + 1) * 32], PD[:], start=False, stop=True), [f2])
            PRr.done(pai, m3)
            PRr.done(pbi, m3b)
            PRr.done(pci, m3c)
            PRr.done(pdi, m4)
            YO, yof, yoi = YOr.get()
            o1 = P.op("dve", lambda e, YO=YO, U=U, b3=b3, p=p, c0=c0: e.scalar_tensor_tensor(out=YO[:], in0=U[:, c0:c0 + SB], scalar=dsk[:, p:p + 1], in1=kb.psum[0:32, b3, :],
                                                                                         op0=ALU.mult, op1=ALU.add), [m4, tu, tdk, yof])
            kb.bank_tok[b3] = o1
            last_dve_u = o1
            td = kb.load(y_d.ap()[p][:, c0:c0 + SB], YO[:], deps=[o1], key=f"yo{yoi}")
            YOr.done(yoi, td)
            kb.out_toks.append(td)
        Ur.done(ui, last_dve_u)
        Ubr.done(ubi, last_pe_u)
        Ctr.done(cti, last_ct)
        Str.done(sti, last_ct)
    return kb.finish()


def scan_params(inp, b, gq):
    g0 = gq * 32
    f = lambda k: np.asarray(inp[k][0], np.float32)
    lam_re, lam_im, log_dt = f("ssm_lam_re"), f("ssm_lam_im"), f("ssm_log_dt")
    b_re, b_im, c_re, c_im, dsk = f("ssm_b_re"), f("ssm_b_im"), f("ssm_c_re"), f("ssm_c_im"), f("ssm_d")
    G = slice(g0, g0 + 32)
    col = lambda a: np.ascontiguousarray(a[G].reshape(NPAIR, 128).T)
    ld_full = np.broadcast_to(log_dt[:, None], (128, 64))
    row = lambda a: np.ascontiguousarray(np.broadcast_to(a[G].reshape(1, NPAIR * 128), (32, NPAIR * 128)))
    bT = np.zeros((32, NPAIR, 128), np.float32)
    bTi = np.zeros((32, NPAIR, 128), np.float32)
    cbd = np.zeros((128, NPAIR, 32), np.float32)
    cbdi = np.zeros((128, NPAIR, 32), np.float32)
    for p in range(NPAIR):
        for gl in range(2):
            g = g0 + 2 * p + gl
            bT[gl * 16:(gl + 1) * 16, p, gl * 64:(gl + 1) * 64] = b_re[g].T
            bTi[gl * 16:(gl + 1) * 16, p, gl * 64:(gl + 1) * 64] = b_im[g].T
            cbd[gl * 64:(gl + 1) * 64, p, gl * 16:(gl + 1) * 16] = c_re[g].T
            cbdi[gl * 64:(gl + 1) * 64, p, gl * 16:(gl + 1) * 16] = c_im[g].T
    return dict(lr_c=col(lam_re), li_c=col(lam_im), ld_c=col(ld_full), lr_r=row(lam_re), li_r=row(lam_im), ld_r=row(ld_full),
                b_re=bT.reshape(32, -1), b_im=bTi.reshape(32, -1), c_re=cbd.reshape(128, -1), c_im=cbdi.reshape(128, -1),
                dskin=np.ascontiguousarray(dsk.reshape(128, 16)[G].reshape(NPAIR, 32).T),
                iotain=np.ascontiguousarray(np.broadcast_to(np.arange(SB + 1, dtype=np.float32)[None, :], (128, SB + 1))))


def run_scan(inp, u, ncore=NCORE):
    maps = []
    for c in range(ncore):
        b, gq = c // 4, c % 4
        m = scan_params(inp, b, gq)
        uc = u[b][:, gq * 512:(gq + 1) * 512]
        m["u"] = np.ascontiguousarray(uc.T.reshape(NPAIR, 32, SEQ))
        maps.append(m)
    res = run_bass_kernel_spmd(get_prog("scan", build_scan), maps, core_ids=list(range(ncore)))
    y = np.zeros_like(u)
    for c in range(ncore):
        b, gq = c // 4, c % 4
        y[b][:, gq * 512:(gq + 1) * 512] = np.asarray(res.results[c]["y"]).reshape(512, SEQ).T
    return y


GELU_C = 2.0 * math.sqrt(2.0 / math.pi)


def build_l2b():
    kb = KB(nslots=4)
    P = kb.P
    xin = P.dram("xT", [128, DC, T], F32, kind="ExternalInput")
    yin = P.dram("yT", [128, DC, T], F32, kind="ExternalInput")
    sgin = P.dram("sgT", [128, DC, T], F32, kind="ExternalInput")
    pin = P.dram("pT", [128, 2, T], F32, kind="ExternalInput")
    w_glu = P.dram("w_glu", [D, 4096], F32, kind="ExternalInput")
    bgl_d = P.dram("b_glu", [128, 32], F32, kind="ExternalInput")
    w_out = P.dram("w_out", [D, D], F32, kind="ExternalInput")
    w_pp = P.dram("ple_proj", [256, D], F32, kind="ExternalInput")
    w_pg = P.dram("ple_gate", [D, D], F32, kind="ExternalInput")
    pn_d = P.dram("ple_norm", [128, DC], F32, kind="ExternalInput")
    ng3_d = P.dram("norm_g3", [128, DC], F32, kind="ExternalInput")
    w3_in = P.dram("w3_in", [D, 8192], F32, kind="ExternalInput")
    w_fg = P.dram("w_fg", [D, 32], F32, kind="ExternalInput")
    nbfg_d = P.dram("b_fg", [32, 1], F32, kind="ExternalInput")
    xout = P.dram("xo", [128, DC, T], F32, kind="ExternalOutput")
    qout = P.dram("qo", [128, DC, T], BF16, kind="ExternalOutput")
    kout = P.dram("ko", [128, DC, T], BF16, kind="ExternalOutput")
    vout = P.dram("vo", [128, DC, T], BF16, kind="ExternalOutput")
    gout = P.dram("go", [128, DC, T], F32, kind="ExternalOutput")
    lfout = P.dram("lfo", [32, T], F32, kind="ExternalOutput")

    X = P.sb("X", [128, DC, T], F32)
    A = P.sb("A", [128, DC, T], BF16)
    Braw = P.sb("Braw", [128, 8192], F32)
    B = Braw[:].bitcast(BF16).rearrange("p (c t) -> p c t", c=DC)
    ST = [Braw[:, i * 1024:(i + 1) * 1024] for i in range(8)]
    pn = P.sb("pn", [128, DC], F32)
    ng3 = P.sb("ng3", [128, DC], F32)
    bgl = P.sb("bgl", [128, 32], F32)
    bfg = P.sb("bfg", [32, 1], F32)
    pT = P.sb("pTb", [128, 2, T], BF16)
    tx = kb.load(X[:], xin.ap())
    tpn = kb.load(pn[:], pn_d.ap())
    tng3 = kb.load(ng3[:], ng3_d.ap())
    tbg = kb.load(bgl[:], bgl_d.ap())
    tbf = kb.load(bfg[:], nbfg_d.ap())
    tp = P.dma("pool", "pcast", pT[:], pin.ap())
    nbf = P.sb("nbf", [32, 1], F32)
    tnbf = P.op("dve", lambda e: e.tensor_scalar(out=nbf[:], in0=bfg[:], scalar1=-1.0, scalar2=None, op0=ALU.mult), [tbf])

    Yr = Rot([P.sb(f"Y{i}", [128, 512], F32) for i in range(2)])
    G1 = Rot([P.sb(f"G1{i}", [128, 512], F32) for i in range(2)])
    tg = None
    for c in range(DC):
        for (h0, hn) in TB2:
            Y, yf, yi = Yr.get()
            ty = kb.load(Y[:], yin.ap()[:, c, h0:h0 + hn], deps=[yf], key=f"y{yi}")
            G, gf, gi = G1.get()
            t = P.op("dve", lambda e, G=G, Y=Y: e.tensor_tensor(out=G[:], in0=Y[:], in1=Y[:], op=ALU.mult), [ty, gf])
            t = P.op("dve", lambda e, G=G: e.tensor_scalar(out=G[:], in0=G[:], scalar1=0.044715, scalar2=1.0, op0=ALU.mult, op1=ALU.add), [t])
            t = P.op("dve", lambda e, G=G, Y=Y: e.tensor_tensor(out=G[:], in0=G[:], in1=Y[:], op=ALU.mult), [t])
            t = P.op("act", lambda e, G=G: e.activation(out=G[:], in_=G[:], func=AF.Sigmoid, scale=GELU_C), [t])
            tg = P.op("dve", lambda e, G=G, Y=Y, c=c, h0=h0, hn=hn: e.tensor_tensor(out=A[:, c, h0:h0 + hn], in0=G[:], in1=Y[:], op=ALU.mult), [t])
            Yr.done(yi, tg)
            G1.done(gi, tg)

    SGr = Rot([P.sb(f"SG{i}", [128, 512], F32) for i in range(3)])
    tA, tB_ = kb.scrv(0), kb.scrv(1)
    stg = {"fa": None, "fb": None, "last": None, "sg": {}}

    def sg_load(j, bi):
        if j < DC and (j, bi) not in stg["sg"]:
            S, sf, si = SGr.get()
            t0, n = TB2[bi]
            t = kb.load(S[:], sgin.ap()[:, j, t0:t0 + n], deps=[sf], key=f"sg{si}")
            stg["sg"][(j, bi)] = (S, si, t)

    sg_load(0, 0)

    def epi_glu(g, ci, bi, pas, toks):
        j = g * 2 + ci
        t0, n = TB2[bi]
        sg_load(j, bi)
        nj, nb = (j, 1) if bi == 0 else (j + 1, 0)
        sg_load(nj, nb)
        S, si, ts = stg["sg"][(j, bi)]
        t1 = P.op("act", lambda e: e.activation(out=tA[:, 0:n], in_=pas[1], func=AF.Sigmoid, bias=bgl[:, 16 + j:17 + j]), [toks[1], stg["fa"], tbg])
        t2 = P.op("dve", lambda e: e.scalar_tensor_tensor(out=tB_[:, 0:n], in0=pas[0], scalar=bgl[:, j:j + 1], in1=tA[:, 0:n], op0=ALU.add, op1=ALU.mult),
                  [t1, toks[0], stg["fb"]])
        t3 = P.op("dve", lambda e: e.tensor_tensor(out=B[:, j, t0:t0 + n], in0=tB_[:, 0:n], in1=S[:, 0:n], op=ALU.mult), [t2, ts])
        stg["fa"] = t2
        stg["fb"] = t3
        stg["last"] = t3
        SGr.done(si, t3)
        return [t2, t1]

    rhsA = lambda kc, t0, n: A[:, kc, t0:t0 + n]
    kb.gemm([dict(W=w_glu, K=D, col=lambda g: g * 256, rhs=rhsA, deps=[tg]),
             dict(W=w_glu, K=D, col=lambda g: 2048 + g * 256, rhs=rhsA, deps=[tg])], 8, TB2, epi_glu)

    tres = kb.outproj_residual(w_out, X, B, TB2, a_deps=stg["last"], x_deps=[tx])
    tple = kb.ple(w_pp, w_pg, 2, pn, X, A, pT, TB2, x_deps=[tres, tpn], a_free=[tres], p_deps=[tp])
    kb.out_toks.append(kb.load(xout.ap(), X[:], deps=[tple]))

    tn3 = kb.rmsnorm(lambda c, t0, n: X[:, c, t0:t0 + n], ng3, lambda c, t0, n: A[:, c, t0:t0 + n], TB2, deps=[tple, tng3], out_deps=[tple])
    sts = {"rr": 0, "free": [None] * 8}

    def make_epi(outs):
        def epi(g, ci, bi, pas, toks):
            j = g * 2 + ci
            t0, n = TB2[bi]
            rel = []
            for si, (func, scale, dst, isbf) in enumerate(outs):
                if bi == 0:
                    i = sts["rr"]
                    sts["rr"] = (i + 1) % 8
                    sts[("slot", si)] = i
                i = sts[("slot", si)]
                S = ST[i].bitcast(BF16)[:, 0:T] if isbf else ST[i]
                t1 = P.op("act", lambda e, S=S, si=si, func=func, scale=scale: e.activation(out=S[:, t0:t0 + n], in_=pas[si], func=func, scale=scale),
                          [toks[si], sts["free"][i], tres])
                rel.append(t1)
                if bi == 1:
                    td = kb.load(dst.ap()[:, j, :], S, deps=[t1], key=f"st{i}")
                    sts["free"][i] = td
                    kb.out_toks.append(td)
            return rel
        return epi

    kb.gemm([dict(W=w3_in, K=D, col=lambda g: g * 256, rhs=rhsA, deps=[tn3]),
             dict(W=w3_in, K=D, col=lambda g: 2048 + g * 256, rhs=rhsA, deps=[tn3])], 8, TB2,
            make_epi([(AF.Copy, 0.125, qout, True), (AF.Copy, 1.0, kout, True)]))
    kb.gemm([dict(W=w3_in, K=D, col=lambda g: 4096 + g * 256, rhs=rhsA, deps=[tn3]),
             dict(W=w3_in, K=D, col=lambda g: 6144 + g * 256, rhs=rhsA, deps=[tn3])], 8, TB2,
            make_epi([(AF.Copy, 1.0, vout, True), (AF.Silu, 1.0, gout, False)]))

    LF = P.sb("LF", [32, T], F32)

    def epi_fg(g, ci, bi, pas, toks):
        t0, n = TB2[bi]
        t1 = P.op("act", lambda e: e.activation(out=LF[:, t0:t0 + n], in_=pas[0], func=AF.Exp, scale=-1.0, bias=nbf[:, 0:1]), [toks[0], tnbf])
        t2 = P.op("act", lambda e: e.activation(out=LF[:, t0:t0 + n], in_=LF[:, t0:t0 + n], func=AF.Ln, bias=kb.cst[0:32, 2:3]), [t1, kb.t_cst])
        t3 = P.op("dve", lambda e: e.tensor_scalar(out=LF[:, t0:t0 + n], in0=LF[:, t0:t0 + n], scalar1=-1.0, scalar2=None, op0=ALU.mult), [t2])
        if bi == 1:
            kb.out_toks.append(kb.load(lfout.ap(), LF[:], deps=[t3]))
        return [t1]

    kb.gemm([dict(W=w_fg, K=D, col=lambda g: 0, rhs=rhsA, deps=[tn3])], 1, TB2, epi_fg, gcols=32)
    return kb.finish()


def run_l2b(inp, x2, y, sg, ncore=NCORE):
    p = np.asarray(inp["p"], np.float32)
    common = dict(w_glu=np.asarray(inp["ssm_w_glu"][0]), b_glu=vec_fm(np.asarray(inp["ssm_b_glu"][0])),
                  w_out=np.asarray(inp["ssm_w_out"][0]),
                  ple_proj=np.asarray(inp["ple_proj"][2]), ple_gate=np.asarray(inp["ple_gate"][2]),
                  ple_norm=vec_fm(np.asarray(inp["ple_norm"][2])), norm_g3=vec_fm(np.asarray(inp["norm_g"][3])),
                  w3_in=np.asarray(inp["fox_w_in"][0]), w_fg=np.asarray(inp["fox_w_fg"][0]),
                  b_fg=np.asarray(inp["fox_b_fg"][0], np.float32).reshape(32, 1))
    maps = []
    for c in range(ncore):
        b, t = c // 4, c % 4
        sl = slice(t * T, (t + 1) * T)
        m = dict(common)
        m["xT"] = fm(x2[b, sl]); m["yT"] = fm(y[b, sl]); m["sgT"] = fm(sg[b, sl]); m["pT"] = fm(p[2, b, sl])
        maps.append(m)
    res = run_bass_kernel_spmd(get_prog("l2b", build_l2b), maps, core_ids=list(range(ncore)))
    B_ = x2.shape[0]
    x3 = np.zeros_like(x2)
    q = np.zeros((B_, SEQ, D), NPBF); k = np.zeros((B_, SEQ, D), NPBF); v = np.zeros((B_, SEQ, D), NPBF)
    g = np.zeros_like(x2)
    lf = np.zeros((B_, SEQ, 32), np.float32)
    for c in range(ncore):
        b, t = c // 4, c % 4
        sl = slice(t * T, (t + 1) * T)
        r = res.results[c]
        x3[b, sl] = unfm(np.asarray(r["xo"])); g[b, sl] = unfm(np.asarray(r["go"]))
        q[b, sl] = unfm(np.asarray(r["qo"])); k[b, sl] = unfm(np.asarray(r["ko"])); v[b, sl] = unfm(np.asarray(r["vo"]))
        lf[b, sl] = np.asarray(r["lfo"]).T
    return x3, q, k, v, g, lf


HPC = 8


def build_fox():
    kb = KB(nslots=1, wcols=128)
    P = kb.P
    q_d = P.dram("q", [HPC, 64, SEQ], BF16, kind="ExternalInput")
    k_d = P.dram("k", [HPC, 64, SEQ], BF16, kind="ExternalInput")
    v_d = P.dram("v", [HPC, 128, 32, 64], BF16, kind="ExternalInput")
    lf_d = P.dram("lf", [HPC, SEQ], F32, kind="ExternalInput")
    dm_d = P.dram("dmask", [128, 4, 512], F32, kind="ExternalInput")
    o_d = P.dram("o", [HPC, 64, SEQ], F32, kind="ExternalOutput")

    LFt = P.sb("LFt", [HPC, SEQ], F32)
    Cc = P.sb("Cc", [HPC, SEQ], F32)
    R1 = P.sb("R1", [HPC, SEQ], F32)
    CH = P.sb("CH", [HPC, 3, SEQ], BF16)
    NCH = P.sb("NCH", [HPC, 3, SEQ], BF16)
    dm = P.sb("dm", [128, 4, 512], F32)
    tl = kb.load(LFt[:], lf_d.ap())
    tdm = kb.load(dm[:], dm_d.ap())
    t = P.op("dve", lambda e: e.tensor_tensor_scan(out=Cc[:], data0=kb.cst[0:HPC, 2:3].to_broadcast([HPC, SEQ]), data1=LFt[:], initial=0.0,
                                                  op0=ALU.mult, op1=ALU.add), [tl, kb.t_cst])
    t = P.op("dve", lambda e: e.tensor_copy(out=CH[:, 0, :], in_=Cc[:]), [t])
    t = P.op("dve", lambda e: e.tensor_tensor(out=R1[:], in0=Cc[:], in1=CH[:, 0, :], op=ALU.subtract), [t])
    t = P.op("dve", lambda e: e.tensor_copy(out=CH[:, 1, :], in_=R1[:]), [t])
    t = P.op("dve", lambda e: e.tensor_tensor(out=R1[:], in0=R1[:], in1=CH[:, 1, :], op=ALU.subtract), [t])
    t = P.op("dve", lambda e: e.tensor_copy(out=CH[:, 2, :], in_=R1[:]), [t])
    tch = P.op("dve", lambda e: e.tensor_scalar(out=NCH[:], in0=CH[:], scalar1=-1.0, scalar2=None, op0=ALU.mult), [t])

    QAr = Rot([P.sb(f"QA{i}", [70, SEQ], BF16) for i in range(2)])
    KAr = Rot([P.sb(f"KA{i}", [70, SEQ], BF16) for i in range(2)])
    VEr = Rot([P.sb(f"VE{i}", [128, 32, 128], BF16) for i in range(2)])
    init_tok = []
    for i in range(2):
        init_tok.append([
            P.op("dve", lambda e, i=i: e.memset(QAr.tiles[i][64:70, :], 1.0)),
            P.op("dve", lambda e, i=i: e.memset(KAr.tiles[i][64:70, :], 1.0)),
            P.op("dve", lambda e, i=i: e.memset(VEr.tiles[i][:], 1.0))])
    Er = Rot([P.sb(f"E{i}", [128, 512], BF16) for i in range(4)])
    Ef = Rot([P.sb(f"Ef{i}", [128, 512], F32) for i in range(2)])
    Rr = Rot([P.sb(f"Rc{i}", [64, 512], F32) for i in range(2)])
    Or = Rot([P.sb(f"Oo{i}", [64, 512], F32) for i in range(2)])
    import os
    FOX_NH = int(os.environ.get("FOX_NH", str(HPC)))
    FOX_NQ = int(os.environ.get("FOX_NQ", "8"))
    FOX_AUG = int(os.environ.get("FOX_AUG", "1"))
    FOX_PH = int(os.environ.get("FOX_PH", "9"))
    fox_rr = {"o": 0, "s": 0}
    for h in range(FOX_NH):
        QA, qf, qi = QAr.get()
        KA, kf, ki = KAr.get()
        VE, vf, vi = VEr.get()
        tq = [kb.load(QA[0:64, :], q_d.ap()[h], deps=[qf, init_tok[qi]], key=f"q{qi}", eng="pool")]
        tk = [kb.load(KA[0:64, :], k_d.ap()[h], deps=[kf, init_tok[ki]], key=f"k{ki}", eng="pool")]
        if FOX_AUG:
            tq.append(kb.load(QA[64:67, :], CH[h:h + 1, :, :], deps=[qf, tch, init_tok[qi]], key=f"qc{qi}", eng="pool"))
            tk.append(kb.load(KA[67:70, :], NCH[h:h + 1, :, :], deps=[kf, tch, init_tok[ki]], key=f"kc{ki}", eng="pool"))
        tv = kb.load(VE[:, :, 0:64], v_d.ap()[h], deps=[vf, init_tok[vi]], key=f"v{vi}", eng="pool")
        last_pe = None
        for qb4 in range(FOX_NQ):
            q0 = qb4 * 512
            nkb = 4 * (qb4 + 1)
            bo = fox_rr["o"]
            fox_rr["o"] = 1 - bo
            bod = kb.bank_tok[bo]
            sinfo = {}

            def emit_s(kbk):
                b = 2 + fox_rr["s"]
                fox_rr["s"] = (fox_rr["s"] + 1) % 6
                bd = kb.bank_tok[b]
                m = P.op("pe", lambda e, b=b, kbk=kbk, KA=KA, QA=QA, q0=q0: e.matmul(kb.psum[:, b, :], KA[0:70, kbk * 128:(kbk + 1) * 128], QA[0:70, q0:q0 + 512], start=True, stop=True),
                         [bd, tq, tk])
                sinfo[kbk] = (b, m)

            emit_s(0)
            if nkb > 1:
                emit_s(1)
            for kbk in range(nkb):
                b, m = sinfo[kbk]
                E, ef, ei = Er.get()
                d = kbk - 4 * qb4
                if d < 0:
                    te = P.op("act", lambda e, E=E, b=b: e.activation(out=E[:], in_=kb.psum[:, b, :], func=AF.Exp), [m, ef])
                    kb.bank_tok[b] = te
                else:
                    F_, ff, fi = Ef.get()
                    t1 = P.op("dve", lambda e, F_=F_, b=b, d=d: e.tensor_tensor(out=F_[:], in0=kb.psum[:, b, :], in1=dm[:, d, :], op=ALU.add), [m, ff, tdm])
                    kb.bank_tok[b] = t1
                    te = P.op("act", lambda e, E=E, F_=F_: e.activation(out=E[:], in_=F_[:], func=AF.Exp), [t1, ef])
                    Ef.done(fi, te)
                pv = P.op("pe", lambda e, E=E, kbk=kbk, nkb=nkb, bo=bo, VE=VE: e.matmul(kb.psum[:, bo, :], VE[:, kbk, :], E[:], start=(kbk == 0), stop=(kbk == nkb - 1)),
                          [te, tv] + ([bod] if kbk == 0 else []))
                Er.done(ei, pv)
                last_pe = pv
                if kbk + 2 < nkb:
                    emit_s(kbk + 2)
            Rt, rf, ri = Rr.get()
            Ot, of, oi = Or.get()
            if FOX_PH >= 9:
                n1 = P.op("dve", lambda e, Rt=Rt, bo=bo: e.reciprocal(out=Rt[:], in_=kb.psum[64:128, bo, :]), [last_pe, rf])
                n2 = P.op("dve", lambda e, Rt=Rt, Ot=Ot, bo=bo: e.tensor_tensor(out=Ot[:], in0=kb.psum[0:64, bo, :], in1=Rt[:], op=ALU.mult), [n1, of])
            else:
                n2 = P.op("act", lambda e, Ot=Ot, bo=bo: e.activation(out=Ot[:], in_=kb.psum[0:64, bo, :], func=AF.Copy), [last_pe, of, rf])
            kb.bank_tok[bo] = n2
            Rr.done(ri, n2)
            td = kb.load(o_d.ap()[h][:, q0:q0 + 512], Ot[:], deps=[n2], key=f"oo{oi}")
            Or.done(oi, td)
            kb.out_toks.append(td)
        QAr.done(qi, last_pe)
        KAr.done(ki, last_pe)
        VEr.done(vi, last_pe)
    return kb.finish()


def run_fox(q, k, v, lf, ncore=NCORE):
    jj = np.arange(128)[:, None, None]
    dd = np.arange(4)[None, :, None]
    qq = np.arange(512)[None, None, :]
    dmask = np.where(qq >= dd * 128 + jj, 0.0, -30000.0).astype(np.float32)
    maps = []
    for c in range(ncore):
        b, hq = c // 4, c % 4
        cs = slice(hq * 512, (hq + 1) * 512)
        m = dict(dmask=dmask)
        m["q"] = np.ascontiguousarray(q[b][:, cs].T.reshape(HPC, 64, SEQ))
        m["k"] = np.ascontiguousarray(k[b][:, cs].T.reshape(HPC, 64, SEQ))
        m["v"] = np.ascontiguousarray(v[b][:, cs].reshape(32, 128, HPC, 64).transpose(2, 1, 0, 3))
        m["lf"] = np.ascontiguousarray(lf[b][:, hq * 8:(hq + 1) * 8].T)
        maps.append(m)
    res = run_bass_kernel_spmd(get_prog("fox", build_fox), maps, core_ids=list(range(ncore)))
    o = np.zeros(q.shape, np.float32)
    for c in range(ncore):
        b, hq = c // 4, c % 4
        o[b][:, hq * 512:(hq + 1) * 512] = np.asarray(res.results[c]["o"]).reshape(512, SEQ).T
    return o


def build_l3b():
    kb = KB(nslots=4)
    P = kb.P
    xin = P.dram("xT", [128, DC, T], F32, kind="ExternalInput")
    oin = P.dram("oT", [128, DC, T], F32, kind="ExternalInput")
    sgin = P.dram("sgT", [128, DC, T], F32, kind="ExternalInput")
    pin = P.dram("pT", [128, 2, T], F32, kind="ExternalInput")
    w_out = P.dram("w_out", [D, D], F32, kind="ExternalInput")
    w_pp = P.dram("ple_proj", [256, D], F32, kind="ExternalInput")
    w_pg = P.dram("ple_gate", [D, D], F32, kind="ExternalInput")
    pn_d = P.dram("ple_norm", [128, DC], F32, kind="ExternalInput")
    fg_d = P.dram("final_g", [128, DC], F32, kind="ExternalInput")
    xout = P.dram("xo", [128, DC, T], F32, kind="ExternalOutput")
    X = P.sb("X", [128, DC, T], F32)
    A = P.sb("A", [128, DC, T], BF16)
    B = P.sb("B", [128, DC, T], BF16)
    pn = P.sb("pn", [128, DC], F32)
    fg = P.sb("fg", [128, DC], F32)
    pT = P.sb("pTb", [128, 2, T], BF16)
    tx = kb.load(X[:], xin.ap())
    tpn = kb.load(pn[:], pn_d.ap())
    tfg = kb.load(fg[:], fg_d.ap())
    tp = P.dma("pool", "pcast", pT[:], pin.ap())
    Orr = Rot([P.sb(f"O{i}", [128, 512], F32) for i in range(2)])
    Srr = Rot([P.sb(f"S{i}", [128, 512], F32) for i in range(2)])
    tb = None
    for c in range(DC):
        for (h0, hn) in TB2:
            O, of, oi = Orr.get()
            S, sf, si = Srr.get()
            t1 = kb.load(O[:], oin.ap()[:, c, h0:h0 + hn], deps=[of], key=f"o{oi}")
            t2 = kb.load(S[:], sgin.ap()[:, c, h0:h0 + hn], deps=[sf], key=f"s{si}")
            tb = P.op("dve", lambda e, O=O, S=S, c=c, h0=h0, hn=hn: e.tensor_tensor(out=B[:, c, h0:h0 + hn], in0=O[:], in1=S[:], op=ALU.mult), [t1, t2])
            Orr.done(oi, tb)
            Srr.done(si, tb)
    tres = kb.outproj_residual(w_out, X, B, TB2, a_deps=tb, x_deps=[tx])
    tple = kb.ple(w_pp, w_pg, 3, pn, X, A, pT, TB2, x_deps=[tres, tpn], a_free=None, p_deps=[tp])
    tfin = kb.rmsnorm(lambda c, t0, n: X[:, c, t0:t0 + n], fg, lambda c, t0, n: X[:, c, t0:t0 + n], TB2, deps=[tple, tfg])
    kb.out_toks.append(kb.load(xout.ap(), X[:], deps=[tfin]))
    return kb.finish()


def run_l3b(inp, x3, o, sg, ncore=NCORE):
    p = np.asarray(inp["p"], np.float32)
    common = dict(w_out=np.asarray(inp["fox_w_out"][0]), ple_proj=np.asarray(inp["ple_proj"][3]), ple_gate=np.asarray(inp["ple_gate"][3]),
                  ple_norm=vec_fm(np.asarray(inp["ple_norm"][3])), final_g=vec_fm(np.asarray(inp["final_g"])))
    maps = []
    for c in range(ncore):
        b, t = c // 4, c % 4
        sl = slice(t * T, (t + 1) * T)
        m = dict(common)
        m["xT"] = fm(x3[b, sl]); m["oT"] = fm(o[b, sl]); m["sgT"] = fm(sg[b, sl]); m["pT"] = fm(p[3, b, sl])
        maps.append(m)
    res = run_bass_kernel_spmd(get_prog("l3b", build_l3b), maps, core_ids=list(range(ncore)))
    out = np.zeros_like(x3)
    for c in range(ncore):
        b, t = c // 4, c % 4
        out[b, t * T:(t + 1) * T] = unfm(np.asarray(res.results[c]["xo"]))
    return out


_CACHE = {}


def get_prog(name, fn):
    if name not in _CACHE:
        _CACHE[name] = fn()
    return _CACHE[name]


REL_BUCKETS = 32
REL_MAX_DIST = 128


def t5_bucket(dist):
    max_exact = REL_BUCKETS // 2
    d = np.maximum(dist, 1).astype(np.float32)
    large = max_exact + (np.log(d / max_exact) / np.log(REL_MAX_DIST / max_exact) * (REL_BUCKETS - max_exact)).astype(np.int32)
    large = np.minimum(large, REL_BUCKETS - 1)
    return np.where(dist < max_exact, dist, large).astype(np.int32)


def run_l0(inp, stop=None, ncore=NCORE):
    x = np.asarray(inp["x"], np.float32)
    p = np.asarray(inp["p"], np.float32)
    qi = np.arange(128)[:, None]
    kj = np.arange(256)[None, :]
    dist = qi + 128 - kj
    band = ((dist >= 0) & (dist < 128)).astype(np.float32)
    bucket = t5_bucket(np.clip(dist, 0, None))
    rel = np.asarray(inp["rel_bias"], np.float32)
    hperm = np.array([kvh * 8 + 2 * i + par for kvh in range(4) for par in range(2) for i in range(4)])
    bias = rel[bucket][:, :, hperm]
    biasT = np.ascontiguousarray(bias.transpose(1, 2, 0).reshape(2, 128, 32, 128).transpose(1, 0, 2, 3))
    maskT = np.ascontiguousarray(band.T.reshape(2, 128, 128).transpose(1, 0, 2))
    sinks = np.ascontiguousarray(np.broadcast_to(np.asarray(inp["swa_sinks"], np.float32)[0][None, :], (128, 32)))
    common = dict(w_in=np.asarray(inp["swa_w_in"][0]), w_out=np.asarray(inp["swa_w_out"][0]),
                  ple_proj=np.asarray(inp["ple_proj"][0]), ple_gate=np.asarray(inp["ple_gate"][0]),
                  norm_g=vec_fm(np.asarray(inp["norm_g"][0])), ple_norm=vec_fm(np.asarray(inp["ple_norm"][0])),
                  biasT=biasT, maskT=maskT, sinks=sinks)
    maps = []
    for c in range(ncore):
        b, t = c // 4, c % 4
        xe = np.zeros((T + 128, D), np.float32)
        xe[128:] = x[b, t * T:(t + 1) * T]
        if t > 0:
            xe[:128] = x[b, t * T - 128:t * T]
        m = dict(common)
        m["xT"] = fm(xe)
        m["pT"] = fm(p[0, b, t * T:(t + 1) * T])
        m["hasprev"] = np.full((128, 1), 1.0 if t > 0 else 0.0, np.float32)
        maps.append(m)
    res = run_bass_kernel_spmd(get_prog("l0" + str(stop), lambda: build_l0(stop)), maps, core_ids=list(range(ncore)))
    x1 = np.zeros_like(x)
    for c in range(ncore):
        b, t = c // 4, c % 4
        x1[b, t * T:(t + 1) * T] = unfm(np.asarray(res.results[c]["xo"]))
    return x1


def kernel(**inputs):
    inp = inputs
    x1 = run_l0(inp)
    x2, u, sg2 = run_l1(inp, x1)
    y = run_scan(inp, u)
    x3, q, k, v, sg3, lf = run_l2b(inp, x2, y, sg2)
    o = run_fox(q, k, v, lf)
    out = run_l3b(inp, x3, o, sg3)
    return out.astype(np.float32)
```
